# Optimizing a Trainium2 kernel written in Bass

```python
import jax, jax.numpy as jnp
from jax import lax
import numpy as np

D_MODEL = 1024
BATCH = 8
SEQ = 4096
DEPTH = 4

CHUNK = 64
N_MIXERS = 3
N_HEADS = 16
HEAD_DIM = D_MODEL // N_HEADS
LEFT_CHUNKS = 8
BAND = (LEFT_CHUNKS + 1) * CHUNK
MAX_REL = 256
N_REL = 2 * MAX_REL + 1
IDX_HEADS = 8
IDX_DIM = 64
TOPK_MAX = 256
B_QBLOCK = 32
B_SPLITS = (D_MODEL, 2 * D_MODEL, 3 * D_MODEL,
            3 * D_MODEL + IDX_HEADS * IDX_DIM,
            3 * D_MODEL + IDX_HEADS * IDX_DIM + IDX_DIM)
B_PROJ = 3 * D_MODEL + IDX_HEADS * IDX_DIM + IDX_DIM + IDX_HEADS
C_QBLOCK = 128
D_FF = 2816
CONV_W = 3
ROPE_THETA = 10000.0
EPS = 1e-6
N_A = (DEPTH + 2) // 3
N_B = (DEPTH + 1) // 3
N_C = DEPTH // 3

kernel_name = "hybrid_chunk_causal_interleaved_trunk"


def rmsnorm(x, g):
    xf = x.astype(jnp.float32)
    y = xf * lax.rsqrt(jnp.mean(xf * xf, axis=-1, keepdims=True) + EPS)
    return (y * g.astype(jnp.float32)).astype(x.dtype)


def rope_tables(seq, dim):
    inv = ROPE_THETA ** (-jnp.arange(0, dim, 2, dtype=jnp.float32) / dim)
    ang = jnp.arange(seq, dtype=jnp.float32)[:, None] * inv[None, :]
    return jnp.cos(ang)[:, None, :], jnp.sin(ang)[:, None, :]


def rope(x, cos, sin):
    half = x.shape[-1] // 2
    c = cos.astype(x.dtype)
    s = sin.astype(x.dtype)
    x1, x2 = x[..., :half], x[..., half:]
    return jnp.concatenate([x1 * c - x2 * s, x2 * c + x1 * s], axis=-1)


def mixer_chunk_relbias(h, w_qkv, q_norm, k_norm, rel_bias, w_o):
    bsz, seq, _ = h.shape
    n_chunks = seq // CHUNK
    pad = LEFT_CHUNKS * CHUNK
    q, k, v = jnp.split((h @ w_qkv).reshape(bsz, seq, 3, N_HEADS, HEAD_DIM), 3, axis=2)
    q = rmsnorm(q[:, :, 0], q_norm)
    k = rmsnorm(k[:, :, 0], k_norm)
    v = v[:, :, 0]
    kp = jnp.pad(k, ((0, 0), (pad, 0), (0, 0), (0, 0)))
    vp = jnp.pad(v, ((0, 0), (pad, 0), (0, 0), (0, 0)))
    rel = jnp.clip(jnp.arange(CHUNK)[:, None] - jnp.arange(BAND)[None, :] + pad,
                   -MAX_REL, MAX_REL) + MAX_REL
    bias = rel_bias.astype(jnp.float32)[:, rel]
    scale = HEAD_DIM ** -0.5
    qc = q.reshape(bsz, n_chunks, CHUNK, N_HEADS, HEAD_DIM).swapaxes(0, 1)

    def one_chunk(args):
        c, q_blk = args
        k_band = lax.dynamic_slice_in_dim(kp, c * CHUNK, BAND, axis=1)
        v_band = lax.dynamic_slice_in_dim(vp, c * CHUNK, BAND, axis=1)
        s = jnp.einsum('bqhd,bkhd->bhqk', q_blk, k_band).astype(jnp.float32) * scale + bias
        valid = (c * CHUNK - pad + jnp.arange(BAND)) >= 0
        s = jnp.where(valid[None, None, None, :], s, -jnp.inf)
        p = jax.nn.softmax(s, axis=-1).astype(v_band.dtype)
        return jnp.einsum('bhqk,bkhd->bqhd', p, v_band)

    o = lax.map(one_chunk, (jnp.arange(n_chunks), qc))
    o = o.swapaxes(0, 1).reshape(bsz, seq, D_MODEL)
    return o @ w_o


def mixer_indexed_sparse(h, w_in, q_norm, k_norm, w_o, cos, sin):
    bsz, seq, _ = h.shape
    q, k, v, qi, ki, wi = jnp.split(h @ w_in, B_SPLITS, axis=-1)
    q = rope(rmsnorm(q.reshape(bsz, seq, N_HEADS, HEAD_DIM), q_norm), cos, sin)
    k = rope(rmsnorm(k.reshape(bsz, seq, N_HEADS, HEAD_DIM), k_norm), cos, sin)
    v = v.reshape(bsz, seq, N_HEADS, HEAD_DIM)
    qi = rope(qi.reshape(bsz, seq, IDX_HEADS, IDX_DIM), cos, sin)
    ki = rope(ki.reshape(bsz, seq, 1, IDX_DIM), cos, sin)[:, :, 0]
    wi = wi.astype(jnp.float32) * IDX_HEADS ** -0.5
    topk = min(TOPK_MAX, seq // 4)
    n_blk = seq // B_QBLOCK
    key_pos = jnp.arange(seq)
    scale = HEAD_DIM ** -0.5

    def to_blocks(a):
        return a.reshape(bsz, n_blk, B_QBLOCK, *a.shape[2:]).swapaxes(0, 1)

    def one_block(args):
        blk, q_b, qi_b, wi_b = args
        t = blk * B_QBLOCK + jnp.arange(B_QBLOCK)
        limit = (t // CHUNK + 1) * CHUNK
        logits = jnp.einsum('bqhd,bsd->bqhs', qi_b, ki).astype(jnp.float32) * IDX_DIM ** -0.5
        score = jnp.einsum('bqh,bqhs->bqs', wi_b, jax.nn.relu(logits))
        adm = key_pos[None, :] < limit[:, None]
        score = jnp.where(adm[None], score, -jnp.inf)
        _, idx = lax.top_k(score, topk)
        sel_ok = idx < limit[None, :, None]
        k_sel = jax.vmap(lambda kb, ib: kb[ib])(k, idx)
        v_sel = jax.vmap(lambda vb, ib: vb[ib])(v, idx)
        s = jnp.einsum('bqhd,bqkhd->bhqk', q_b, k_sel).astype(jnp.float32) * scale
        s = jnp.where(sel_ok[:, None], s, -jnp.inf)
        p = jax.nn.softmax(s, axis=-1).astype(v_sel.dtype)
        return jnp.einsum('bhqk,bqkhd->bqhd', p, v_sel)

    o = lax.map(one_block, (jnp.arange(n_blk), to_blocks(q), to_blocks(qi), to_blocks(wi)))
    o = o.swapaxes(0, 1).reshape(bsz, seq, D_MODEL)
    return o @ w_o


def mixer_stick_breaking(h, w_qkv, w_o):
    bsz, seq, _ = h.shape
    q, k, v = jnp.split((h @ w_qkv).reshape(bsz, seq, 3, N_HEADS, HEAD_DIM), 3, axis=2)
    q, k, v = q[:, :, 0], k[:, :, 0], v[:, :, 0]
    n_blk = seq // C_QBLOCK
    key_pos = jnp.arange(seq)
    scale = HEAD_DIM ** -0.5
    qb = q.reshape(bsz, n_blk, C_QBLOCK, N_HEADS, HEAD_DIM).swapaxes(0, 1)

    def one_block(args):
        blk, q_b = args
        t = blk * C_QBLOCK + jnp.arange(C_QBLOCK)
        z = jnp.einsum('bqhd,bshd->bhqs', q_b, k).astype(jnp.float32) * scale
        causal = (key_pos[None, :] < t[:, None])[None, None]
        log_beta = jax.nn.log_sigmoid(z)
        log_keep = jnp.where(causal, jax.nn.log_sigmoid(-z), 0.0)
        rev = lax.cumsum(log_keep, axis=3, reverse=True)
        after = jnp.concatenate([rev[..., 1:], jnp.zeros_like(rev[..., :1])], axis=-1)
        a = jnp.where(causal, jnp.exp(log_beta + after), 0.0)
        return jnp.einsum('bhqs,bshd->bqhd', a.astype(v.dtype), v)

    o = lax.map(one_block, (jnp.arange(n_blk), qb))
    o = o.swapaxes(0, 1).reshape(bsz, seq, D_MODEL)
    return o @ w_o


def conv_ffn(h, w_in, conv_w, conv_b, w_down):
    seq = h.shape[1]
    a = h @ w_in
    ap = jnp.pad(a, ((0, 0), (CONV_W - 1, 0), (0, 0)))
    c = conv_b + sum(ap[:, i:i + seq] * conv_w[i] for i in range(CONV_W))
    g, u = jnp.split(c, 2, axis=-1)
    return (jax.nn.silu(g) * u) @ w_down


def setup_inputs(seed: int = 0) -> dict:
    key = jax.random.key(seed)
    ks = jax.random.split(key, 20)
    f32 = jnp.float32
    D = D_MODEL

    def w(k, shape, fan_in):
        return jax.random.normal(k, shape, f32) * fan_in ** -0.5

    def gain(k, shape):
        return 1.0 + 0.1 * jax.random.normal(k, shape, f32)

    return {
        "x": jax.random.normal(ks[0], (BATCH, SEQ, D), f32),
        "norm1_g": gain(ks[1], (DEPTH, D)),
        "norm2_g": gain(ks[2], (DEPTH, D)),
        "a_w_qkv": w(ks[3], (N_A, D, 3 * D), D),
        "a_q_norm": gain(ks[4], (N_A, HEAD_DIM)),
        "a_k_norm": gain(ks[5], (N_A, HEAD_DIM)),
        "a_rel_bias": 0.1 * jax.random.normal(ks[6], (N_A, N_HEADS, N_REL), f32),
        "a_w_o": w(ks[7], (N_A, D, D), D),
        "b_w_in": w(ks[8], (N_B, D, B_PROJ), D),
        "b_q_norm": gain(ks[9], (N_B, HEAD_DIM)),
        "b_k_norm": gain(ks[10], (N_B, HEAD_DIM)),
        "b_w_o": w(ks[11], (N_B, D, D), D),
        "c_w_qkv": w(ks[12], (N_C, D, 3 * D), D),
        "c_w_o": w(ks[13], (N_C, D, D), D),
        "ffn_w_in": w(ks[14], (DEPTH, D, 2 * D_FF), D),
        "ffn_conv_w": w(ks[15], (DEPTH, CONV_W, 2 * D_FF), CONV_W),
        "ffn_conv_b": 0.01 * jax.random.normal(ks[16], (DEPTH, 2 * D_FF), f32),
        "ffn_w_down": w(ks[17], (DEPTH, D_FF, D), D_FF),
    }


def reference(x, norm1_g, norm2_g, a_w_qkv, a_q_norm, a_k_norm, a_rel_bias, a_w_o,
              b_w_in, b_q_norm, b_k_norm, b_w_o, c_w_qkv, c_w_o,
              ffn_w_in, ffn_conv_w, ffn_conv_b, ffn_w_down):
    seq = x.shape[1]
    cos, sin = rope_tables(seq, HEAD_DIM)
    ia = ib = ic = 0
    for layer in range(DEPTH):
        h = rmsnorm(x, norm1_g[layer])
        kind = layer % N_MIXERS
        if kind == 0:
            y = mixer_chunk_relbias(h, a_w_qkv[ia], a_q_norm[ia], a_k_norm[ia], a_rel_bias[ia], a_w_o[ia])
            ia += 1
        elif kind == 1:
            y = mixer_indexed_sparse(h, b_w_in[ib], b_q_norm[ib], b_k_norm[ib], b_w_o[ib], cos, sin)
            ib += 1
        else:
            y = mixer_stick_breaking(h, c_w_qkv[ic], c_w_o[ic])
            ic += 1
        x = x + y
        h = rmsnorm(x, norm2_g[layer])
        x = x + conv_ffn(h, ffn_w_in[layer], ffn_conv_w[layer], ffn_conv_b[layer], ffn_w_down[layer])
    return x
```

```python
import contextlib
import numpy as np
import concourse.bass as bass
import concourse.mybir as mybir
from concourse.bass_utils import run_bass_kernel_spmd

F32 = mybir.dt.float32
BF16 = mybir.dt.bfloat16
AF = mybir.ActivationFunctionType
ALU = mybir.AluOpType
AX = mybir.AxisListType

S_LEN = 4096
D = 1024
DC = 8
NG = 8
GW = 512
DFF = 2816
NFC = 44
NJ = 22
EPS = 1e-6
NEG = -30000.0
N_BISECT = 16
TOPK = 256


class Buf:
    __slots__ = ("name", "w", "r")

    def __init__(self, name):
        self.name = name
        self.w = None
        self.r = []


class Op:
    __slots__ = ("eng", "fn", "deps", "is_dma", "signal", "token", "slot", "idx")


class Sched:
    ENGS = ("pe", "act", "dve", "pool", "sp")
    KSLOT = 8

    def __init__(self, nc, es):
        self.nc = nc
        self.e = {"pe": nc.tensor, "act": nc.scalar, "dve": nc.vector, "pool": nc.gpsimd, "sp": nc.sync}
        self.ops = []
        self.last = {e: None for e in self.ENGS}
        self.dma_since = []
        self.dma_hist = {"sp": [], "pool": []}
        self.bufs = {}
        self.sem = {e: es.enter_context(nc.semaphore("s_" + e)) for e in self.ENGS}
        self.dsem = {q: [es.enter_context(nc.semaphore("d_%s%d" % (q, i))) for i in range(self.KSLOT)]
                     for q in ("sp", "pool")}

    def B(self, *key):
        b = self.bufs.get(key)
        if b is None:
            b = Buf(key)
            self.bufs[key] = b
        return b

    def _add(self, eng, fn, r, w, is_dma):
        op = Op()
        op.eng = eng
        op.fn = fn
        op.is_dma = is_dma
        op.signal = is_dma
        op.token = None
        op.slot = None
        op.idx = len(self.ops)
        deps = set()
        for b in list(r) + list(w):
            if b.w is not None:
                deps.add(b.w)
        for b in w:
            for x in b.r:
                deps.add(x)
        real = set()
        for d in deps:
            dop = self.ops[d]
            if dop.is_dma or is_dma:
                real.add(d)
            elif dop.eng != eng:
                real.add(d)
            else:
                if eng != "pe" and any(b.w == d for b in r):
                    real.add(d)
        if is_dma:
            h = self.dma_hist[eng]
            if len(h) >= self.KSLOT:
                real.add(h[-self.KSLOT])
            op.slot = len(h) % self.KSLOT
            h.append(op.idx)
            self.dma_since.append(op.idx)
        op.deps = real
        for d in real:
            self.ops[d].signal = True
        self.ops.append(op)
        for b in w:
            b.w = op.idx
            b.r = []
        for b in r:
            if b not in w:
                b.r.append(op.idx)
        if not is_dma:
            self.last[eng] = op.idx
        return op

    def op(self, eng, fn, r=(), w=()):
        return self._add(eng, fn, r, w, False)

    def dma(self, q, fn, r=(), w=()):
        return self._add(q, fn, r, w, True)

    def barrier(self):
        lasts = [v for v in self.last.values() if v is not None]
        dmas = list(self.dma_since)
        for eng in self.ENGS:
            op = Op()
            op.eng = eng
            op.fn = None
            op.is_dma = False
            op.signal = False
            op.token = None
            op.slot = None
            op.idx = len(self.ops)
            op.deps = set(d for d in lasts if self.ops[d].eng != eng) | set(dmas)
            for d in op.deps:
                self.ops[d].signal = True
            self.ops.append(op)
        self.dma_since = []
        for b in self.bufs.values():
            b.w = None
            b.r = []

    def emit(self):
        cnt = {e: 0 for e in self.ENGS}
        dcnt = {q: [0] * self.KSLOT for q in ("sp", "pool")}
        seen = {e: {} for e in self.ENGS}
        for op in self.ops:
            eng = self.e[op.eng]
            need = {}
            for d in op.deps:
                sem, val = self.ops[d].token
                k = id(sem)
                if k not in need or need[k][1] < val:
                    need[k] = (sem, val)
            sn = seen[op.eng]
            for k, (sem, val) in need.items():
                if sn.get(k, 0) >= val:
                    continue
                sn[k] = val
                eng.wait_ge(sem, val)
            if op.fn is None:
                continue
            ins = op.fn()
            if op.is_dma:
                dcnt[op.eng][op.slot] += 16
                sem = self.dsem[op.eng][op.slot]
                ins.then_inc(sem, 16)
                op.token = (sem, dcnt[op.eng][op.slot])
            elif op.signal:
                cnt[op.eng] += 1
                ins.then_inc(self.sem[op.eng], 1)
                op.token = (self.sem[op.eng], cnt[op.eng])
        for q in ("sp", "pool"):
            for i in range(self.KSLOT):
                if dcnt[q][i]:
                    self.e[q].wait_ge(self.dsem[q][i], dcnt[q][i])


class Rot:
    def __init__(self, items):
        self.items = items
        self.i = 0

    def next(self):
        it = self.items[self.i % len(self.items)]
        self.i += 1
        return it


def host_consts():
    c = {}
    c["ident_f"] = np.eye(128, dtype=np.float32)
    c["ones"] = np.ones((128, 128), np.float32)
    bo = np.zeros((128, 128), np.float32)
    bo[:64, :64] = 1
    bo[64:, 64:] = 1
    c["blockones"] = bo
    pm = np.zeros((128, 128), np.float32)
    for m in range(128):
        if m % 64 < 32:
            pm[m + 32, m] = -1.0
        else:
            pm[m - 32, m] = 1.0
    c["pmat"] = pm
    j = np.arange(128)[:, None]
    s = np.arange(128)[None, :]
    c["negu"] = -(j >= s).astype(np.float32)
    c["negones"] = -np.ones((128, 128), np.float32)
    ss = np.arange(128)[:, None]
    tt = np.arange(128)[None, :]
    c["tri01"] = (tt > ss).astype(np.float32)
    c["negtri"] = np.where(tt > ss, 0.0, NEG).astype(np.float32)
    inv = 10000.0 ** (-np.arange(0, 64, 2, dtype=np.float32) / 64)
    ang = np.arange(S_LEN, dtype=np.float32)[:, None] * inv[None, :]
    cs = np.cos(ang).astype(np.float32).T
    sn = np.sin(ang).astype(np.float32).T
    c["rcos"] = np.ascontiguousarray(np.tile(cs, (4, 1)))
    c["rsin"] = np.ascontiguousarray(np.tile(sn, (4, 1)))
    return c


def strip_index():
    s = np.arange(128)[:, None]
    u = np.arange(640)[None, :]
    idx = np.clip(u - s, -256, 256) + 256
    band = (u // 64 - 8 <= s // 64) & (s // 64 <= u // 64)
    return idx, band


CONST_NAMES = ["ident_f", "ones", "blockones", "pmat", "negu", "negones", "tri01", "negtri"]
SM_N1 = 0
SM_N2 = 32
SM_CW = 64
SM_CB = SM_CW + 4 * 3 * 44
SM_QK = SM_CB + 4 * 44
SM_NS = SM_QK + 8


def pack_smalls(inp):
    sm = np.zeros((128, SM_NS), np.float32)
    for L in range(4):
        sm[:, SM_N1 + 8 * L:SM_N1 + 8 * L + 8] = inp["norm1_g"][L].reshape(8, 128).T
        sm[:, SM_N2 + 8 * L:SM_N2 + 8 * L + 8] = inp["norm2_g"][L].reshape(8, 128).T
        for i in range(3):
            o = SM_CW + (L * 3 + i) * 44
            sm[:, o:o + 44] = inp["ffn_conv_w"][L, i].reshape(44, 128).T
        o = SM_CB + L * 44
        sm[:, o:o + 44] = inp["ffn_conv_b"][L].reshape(44, 128).T
    qk = {0: (inp["a_q_norm"][0], inp["a_k_norm"][0]), 1: (inp["b_q_norm"][0], inp["b_k_norm"][0]),
          3: (inp["a_q_norm"][1], inp["a_k_norm"][1])}
    for L, (qg, kg) in qk.items():
        sm[:, SM_QK + 2 * L] = np.tile(qg, 2)
        sm[:, SM_QK + 2 * L + 1] = np.tile(kg, 2)
    return sm


def build(n_layers=4):
    nc = bass.Bass("TRN2", target_bir_lowering=False)
    dt = nc.dram_tensor
    x_in = dt("x", [S_LEN, D], F32, kind="ExternalInput").ap()
    y_out = dt("y", [S_LEN, D], F32, kind="ExternalOutput").ap()
    w_qkv = {0: dt("wqkv0", [D, 3 * D], F32, kind="ExternalInput").ap(),
             1: dt("wqkv1", [D, 3656], F32, kind="ExternalInput").ap(),
             2: dt("wqkv2", [D, 3 * D], F32, kind="ExternalInput").ap(),
             3: dt("wqkv3", [D, 3 * D], F32, kind="ExternalInput").ap()}
    w_o = {L: dt("wo%d" % L, [D, D], F32, kind="ExternalInput").ap() for L in range(4)}
    w_in = {L: dt("win%d" % L, [D, 2 * DFF], F32, kind="ExternalInput").ap() for L in range(4)}
    w_dn = {L: dt("wdn%d" % L, [DFF, D], F32, kind="ExternalInput").ap() for L in range(4)}
    strips = {L: dt("strip%d" % L, [128, 16, 640], F32, kind="ExternalInput").ap() for L in (0, 3)}
    smalls_d = dt("smalls", [128, SM_NS], F32, kind="ExternalInput").ap()
    consts_d = {n: dt("c_" + n, [128, 128], F32, kind="ExternalInput").ap() for n in CONST_NAMES}
    rcos_d = dt("rcos", [128, S_LEN], F32, kind="ExternalInput").ap()
    rsin_d = dt("rsin", [128, S_LEN], F32, kind="ExternalInput").ap()
    xT = dt("xT", [DC, 128, S_LEN], F32).ap()
    hT = dt("hT", [DC, 128, S_LEN], BF16).ap()
    qT = dt("qT", [DC, 128, S_LEN], BF16).ap()
    kT = dt("kT", [DC, 128, S_LEN], BF16).ap()
    vS = dt("vS", [S_LEN, D], BF16).ap()
    oT = dt("oT", [DC, 128, S_LEN], BF16).ap()
    vS2 = dt("vS2", [8, 128, 32, 192], BF16).ap()
    qiT = dt("qiT", [4, 128, S_LEN], BF16).ap()
    kiT = dt("kiT", [128, S_LEN], BF16).ap()
    wiS = dt("wiS", [S_LEN, 8], F32).ap()
    nmS = dt("nmS", [32, 128, S_LEN], BF16).ap()

    with contextlib.ExitStack() as es:
        S = Sched(nc, es)
        B = S.B

        _uniq = [0]

        def sb(st, name, shape, dtype):
            _uniq[0] += 1
            return st.enter_context(nc.sbuf_tensor("sb%d_%s" % (_uniq[0], name), shape, dtype))

        smalls = sb(es, "smalls", [128, SM_NS], F32)
        cb = {n: sb(es, "cb_" + n, [128, 128], BF16) for n in CONST_NAMES if n != "ident_f"}
        ident_f = sb(es, "ident_f", [128, 128], F32)
        tri01_f = sb(es, "tri01_f", [128, 128], F32)
        epsb = sb(es, "epsb", [128, 1], F32)
        ppairs = [es.enter_context(nc.psum_tensor("ppair%d" % i, [128, 1024], F32)) for i in range(4)]
        banks = [ppairs[i // 2][:, (i % 2) * 512:(i % 2 + 1) * 512] for i in range(8)]
        bankB = [B("bank", i) for i in range(8)]
        CB = B("consts")
        S.dma("sp", lambda: nc.sync.dma_start(out=smalls[:], in_=smalls_d[:, :]), w=[CB])
        S.dma("sp", lambda: nc.sync.dma_start(out=ident_f[:], in_=consts_d["ident_f"][:, :]), w=[CB])
        S.dma("sp", lambda: nc.sync.dma_start(out=tri01_f[:], in_=consts_d["tri01"][:, :]), w=[CB])
        for n in CONST_NAMES:
            if n == "ident_f":
                continue
            S.dma("pool", (lambda n=n: nc.gpsimd.dma_start(out=cb[n][:], in_=consts_d[n][:, :])), w=[CB])
        S.op("dve", lambda: nc.vector.memset(epsb[:], EPS), w=[CB])
        ident_b = sb(es, "ident_b", [128, 128], BF16)
        S.dma("pool", lambda: nc.gpsimd.dma_start(out=ident_b[:], in_=consts_d["ident_f"][:, :]), w=[CB])
        S.barrier()
        with contextlib.ExitStack() as ist:
            ones_blk = sb(ist, "ones_blk", [128, 32, 64], BF16)
            S.op("dve", lambda: nc.vector.memset(ones_blk[:], 1.0), w=[B("ones_blk")])
            for hp_ in range(8):
                S.dma("sp", lambda hp_=hp_: nc.sync.dma_start(out=vS2[hp_, :, :, 64:128], in_=ones_blk[:]),
                      r=[B("ones_blk")], w=[B("vS2ones", hp_)])
            S.barrier()

        def psum_pool(ids):
            return Rot([(banks[i], bankB[i]) for i in ids])

        def wload(dst_ap, src_ap, wb):
            S.dma("pool", lambda: nc.gpsimd.dma_start(out=dst_ap, in_=src_ap, max_dma_last_dim=4096), w=[wb])

        def pass_norm(gcol, first):
            with contextlib.ExitStack() as st:
                xt = [sb(st, "n_x%d" % i, [128, DC, GW], F32) for i in range(2)]
                sq = [sb(st, "n_sq%d" % i, [128, DC, GW], BF16) for i in range(2)]
                hh = [sb(st, "n_h%d" % i, [128, DC, GW], BF16) for i in range(2)]
                rs = [sb(st, "n_rs%d" % i, [128, GW], F32) for i in range(2)]
                xin = [sb(st, "n_xin%d" % i, [128, 4, D], F32) for i in range(2)] if first else None
                pp = psum_pool(range(8))
                for g in range(NG):
                    i = g % 2
                    cols = slice(g * GW, (g + 1) * GW)
                    bx, bsq, bh, brs = B("n_x", i), B("n_sq", i), B("n_h", i), B("n_rs", i)
                    if first:
                        bxi = B("n_xin", i)
                        S.dma("sp", lambda i=i, g=g: nc.sync.dma_start(
                            out=xin[i][:], in_=x_in[g * GW:(g + 1) * GW, :].rearrange("(t p) d -> p t d", p=128)),
                            w=[bxi])
                        for dc in range(DC):
                            bk, bkb = pp.next()
                            for tt in range(4):
                                S.op("pe", lambda i=i, dc=dc, tt=tt, bk=bk: nc.tensor.transpose(
                                    bk[:, tt * 128:(tt + 1) * 128], xin[i][:, tt, dc * 128:(dc + 1) * 128],
                                    ident_f[:]), r=[bxi, CB], w=[bkb])
                            S.op("act", lambda i=i, dc=dc, bk=bk: nc.scalar.copy(out=xt[i][:, dc, :], in_=bk[:]),
                                 r=[bkb], w=[bx])
                        S.dma("sp", lambda i=i, cols=cols: nc.sync.dma_start(
                            out=xT[:, :, cols].rearrange("c p t -> p c t"), in_=xt[i][:]), r=[bx], w=[B("xT", g)])
                    else:
                        S.dma("sp", lambda i=i, cols=cols: nc.sync.dma_start(
                            out=xt[i][:], in_=xT[:, :, cols].rearrange("c p t -> p c t")), w=[bx])
                    S.op("act", lambda i=i: nc.scalar.activation(out=sq[i][:], in_=xt[i][:], func=AF.Square),
                         r=[bx], w=[bsq])
                    bk, bkb = pp.next()
                    for dc in range(DC):
                        S.op("pe", lambda i=i, dc=dc, bk=bk: nc.tensor.matmul(
                            bk[:], cb["ones"][:], sq[i][:, dc, :], start=(dc == 0), stop=(dc == DC - 1)),
                            r=[bsq, CB], w=[bkb])
                    S.op("act", lambda i=i, bk=bk: nc.scalar.activation(
                        out=rs[i][:], in_=bk[:], func=AF.Ln, bias=epsb[:, 0:1], scale=1.0 / D), r=[bkb, CB], w=[brs])
                    S.op("act", lambda i=i: nc.scalar.activation(
                        out=rs[i][:], in_=rs[i][:], func=AF.Exp, scale=-0.5), r=[brs], w=[brs])
                    for dc in range(DC):
                        S.op("dve", lambda i=i, dc=dc: nc.vector.scalar_tensor_tensor(
                            hh[i][:, dc, :], xt[i][:, dc, :], smalls[:, gcol + dc:gcol + dc + 1], rs[i][:],
                            ALU.mult, ALU.mult), r=[bx, brs, CB], w=[bh])
                    S.dma("sp", lambda i=i, cols=cols: nc.sync.dma_start(
                        out=hT[:, :, cols].rearrange("c p t -> p c t"), in_=hh[i][:]), r=[bh], w=[B("hT", g)])
            S.barrier()

        def load_qkv_w(st, L, kind):
            wsrc = w_qkv[L]
            ncolw = 3 * D + (640 if kind == 1 else 0)
            wt = sb(st, "q_w", [128, DC, ncolw], BF16)
            WB = B("q_w")
            wwi = None
            for dc in range(DC):
                rows = slice(dc * 128, (dc + 1) * 128)
                wload(wt[:, dc, 0:3 * D], wsrc[rows, 0:3 * D], WB)
                if kind == 1:
                    wload(wt[:, dc, 3072:3584], wsrc[rows, 3072:3584], WB)
                    wload(wt[:, dc, 3584:3648], wsrc[rows, 3584:3648], WB)
                    wload(wt[:, dc, 3648:3712], wsrc[rows, 3584:3648], WB)
            if kind == 1:
                wwi = sb(st, "q_wwi", [128, DC, 8], BF16)
                for dc in range(DC):
                    wload(wwi[:, dc, :], wsrc[dc * 128:(dc + 1) * 128, 3648:3656], WB)
            return wt, wwi

        def pass_qkv(L, kind, wt, wwi):
            WB = B("q_w")
            with contextlib.ExitStack() as st:
                hh = [sb(st, "q_h%d" % i, [128, DC, GW], BF16) for i in range(2)]
                sqt = [sb(st, "q_sq%d" % i, [128, GW], BF16) for i in range(3)]
                rst = [sb(st, "q_rs%d" % i, [128, GW], F32) for i in range(3)]
                xnt = [sb(st, "q_xn%d" % i, [128, GW], BF16) for i in range(3)]
                t1t = [sb(st, "q_t1%d" % i, [128, GW], F32) for i in range(3)]
                t2t = [sb(st, "q_t2%d" % i, [128, GW], F32) for i in range(3)]
                outt = [sb(st, "q_o%d" % i, [128, GW], BF16) for i in range(4)]
                vt = [sb(st, "q_v%d" % i, [128, D], BF16) for i in range(2)]
                wit = [sb(st, "q_wi%d" % i, [128, 8], F32) for i in range(2)]
                qk8 = sb(st, "q_qk8", [128, 1], F32)
                if kind == 1:
                    rc = [sb(st, "q_rc%d" % i, [128, GW], F32) for i in range(2)]
                    rsn = [sb(st, "q_rsn%d" % i, [128, GW], F32) for i in range(2)]
                r_sq, r_rs, r_xn, r_t1, r_t2, r_out = (
                    Rot([0, 1, 2]), Rot([0, 1, 2]), Rot([0, 1, 2]), Rot([0, 1, 2]), Rot([0, 1, 2]), Rot([0, 1, 2, 3]))
                pp = psum_pool(range(8))
                if kind != 2:
                    S.op("dve", lambda: nc.vector.tensor_scalar(
                        qk8[:], smalls[:, SM_QK + 2 * L:SM_QK + 2 * L + 1], 0.125, None, ALU.mult),
                        r=[CB], w=[B("q_qk8")])
                def load_h(g):
                    i = g % 2
                    cols = slice(g * GW, (g + 1) * GW)
                    S.dma("sp", lambda: nc.sync.dma_start(
                        out=hh[i][:], in_=hT[:, :, cols].rearrange("c p t -> p c t")), w=[B("q_h", i)])
                    if kind == 1:
                        S.dma("sp", lambda: nc.sync.dma_start(out=rc[i][:], in_=rcos_d[:, cols]), w=[B("q_rc", i)])
                        S.dma("sp", lambda: nc.sync.dma_start(out=rsn[i][:], in_=rsin_d[:, cols]), w=[B("q_rc", i)])

                load_h(0)
                for g in range(NG):
                    i = g % 2
                    cols = slice(g * GW, (g + 1) * GW)
                    bh = B("q_h", i)
                    if kind == 1:
                        brc = B("q_rc", i)
                    if g + 1 < NG:
                        load_h(g + 1)
                    fm = []
                    for c in range(8):
                        fm.append((qT, c, c * 128, "q"))
                    for c in range(8):
                        fm.append((kT, c, D + c * 128, "k"))
                    if kind == 1:
                        for c in range(4):
                            fm.append((qiT, c, 3072 + c * 128, "qi"))
                        fm.append((kiT, None, 3584, "ki"))
                    for (dst, c, w0, role) in fm:
                        bk, bkb = pp.next()
                        for dc in range(DC):
                            S.op("pe", lambda i=i, dc=dc, bk=bk, w0=w0: nc.tensor.matmul(
                                bk[:], wt[:, dc, w0:w0 + 128], hh[i][:, dc, :], start=(dc == 0), stop=(dc == DC - 1)),
                                r=[bh, WB], w=[bkb])
                        io = r_out.next()
                        bo = B("q_o", io)
                        normed = (kind != 2) and role in ("q", "k")
                        roped = (kind == 1)
                        if normed:
                            isq, irs, ixn = r_sq.next(), r_rs.next(), r_xn.next()
                            bsq, brs, bxn = B("q_sq", isq), B("q_rs", irs), B("q_xn", ixn)
                            S.op("act", lambda isq=isq, bk=bk: nc.scalar.activation(
                                out=sqt[isq][:], in_=bk[:], func=AF.Square), r=[bkb], w=[bsq])
                            bk2, bkb2 = pp.next()
                            S.op("pe", lambda isq=isq, bk2=bk2: nc.tensor.matmul(
                                bk2[:], cb["blockones"][:], sqt[isq][:], start=True, stop=True),
                                r=[bsq, CB], w=[bkb2])
                            S.op("act", lambda irs=irs, bk2=bk2: nc.scalar.activation(
                                out=rst[irs][:], in_=bk2[:], func=AF.Ln, bias=epsb[:, 0:1], scale=1.0 / 64),
                                r=[bkb2, CB], w=[brs])
                            S.op("act", lambda irs=irs: nc.scalar.activation(
                                out=rst[irs][:], in_=rst[irs][:], func=AF.Exp, scale=-0.5), r=[brs], w=[brs])
                            gain = qk8[:, 0:1] if role == "q" else smalls[:, SM_QK + 2 * L + 1:SM_QK + 2 * L + 2]
                            tgt = xnt[ixn] if roped else outt[io]
                            btgt = bxn if roped else bo
                            S.op("dve", lambda tgt=tgt, bk=bk, gain=gain, irs=irs: nc.vector.scalar_tensor_tensor(
                                tgt[:], bk[:], gain, rst[irs][:], ALU.mult, ALU.mult),
                                r=[bkb, brs, B("q_qk8"), CB], w=[btgt])
                        elif roped:
                            ixn = r_xn.next()
                            bxn = B("q_xn", ixn)
                            S.op("act", lambda ixn=ixn, bk=bk: nc.scalar.copy(out=xnt[ixn][:], in_=bk[:]),
                                 r=[bkb], w=[bxn])
                        else:
                            sc = 0.125 if role == "q" else 1.0
                            S.op("act", lambda io=io, bk=bk, sc=sc: nc.scalar.mul(out=outt[io][:], in_=bk[:], mul=sc),
                                 r=[bkb], w=[bo])
                        if roped:
                            it1, it2 = r_t1.next(), r_t2.next()
                            bt1, bt2 = B("q_t1", it1), B("q_t2", it2)
                            bk3, bkb3 = pp.next()
                            S.op("pe", lambda ixn=ixn, bk3=bk3: nc.tensor.matmul(
                                bk3[:], cb["pmat"][:], xnt[ixn][:], start=True, stop=True), r=[bxn, CB], w=[bkb3])
                            S.op("pool", lambda it1=it1, ixn=ixn, i=i: nc.gpsimd.tensor_tensor(
                                t1t[it1][:], xnt[ixn][:], rc[i][:], ALU.mult), r=[bxn, brc], w=[bt1])
                            S.op("dve", lambda it2=it2, bk3=bk3, i=i: nc.vector.tensor_tensor(
                                t2t[it2][:], bk3[:], rsn[i][:], ALU.mult), r=[bkb3, brc], w=[bt2])
                            S.op("pool", lambda io=io, it1=it1, it2=it2: nc.gpsimd.tensor_tensor(
                                outt[io][:], t1t[it1][:], t2t[it2][:], ALU.add), r=[bt1, bt2], w=[bo])
                        if role == "ki":
                            S.dma("sp", lambda io=io, cols=cols: nc.sync.dma_start(
                                out=kiT[:, cols], in_=outt[io][:]), r=[bo], w=[B("kiT", g)])
                        else:
                            S.dma("sp", lambda io=io, cols=cols, dst=dst, c=c: nc.sync.dma_start(
                                out=dst[c, :, cols], in_=outt[io][:]), r=[bo], w=[B("fm", role, c, g)])
                    for tt in range(4):
                        iv = (g * 4 + tt) % 2
                        bv = B("q_v", iv)
                        for half in range(2):
                            bk, bkb = pp.next()
                            for dc in range(DC):
                                S.op("pe", lambda i=i, dc=dc, bk=bk, tt=tt, half=half: nc.tensor.matmul(
                                    bk[:], hh[i][:, dc, tt * 128:(tt + 1) * 128],
                                    wt[:, dc, 2 * D + half * 512:2 * D + (half + 1) * 512],
                                    start=(dc == 0), stop=(dc == DC - 1)), r=[bh, WB], w=[bkb])
                            S.op("act", lambda iv=iv, bk=bk, half=half: nc.scalar.copy(
                                out=vt[iv][:, half * 512:(half + 1) * 512], in_=bk[:]), r=[bkb], w=[bv])
                        t0 = g * GW + tt * 128
                        if kind == 2:
                            S.dma("sp", lambda iv=iv, t0=t0: nc.sync.dma_start(out=vS[t0:t0 + 128, :], in_=vt[iv][:]),
                                  r=[bv], w=[B("vS", g, tt)])
                        else:
                            kb_ = g * 4 + tt
                            for e in range(2):
                                S.dma("sp", lambda iv=iv, kb_=kb_, e=e: nc.sync.dma_start(
                                    out=vS2[:, :, kb_, 128 * e:128 * e + 64].rearrange("h p f -> p h f"),
                                    in_=vt[iv][:].rearrange("p (h x) -> p h x", x=128)[:, :, 64 * e:64 * e + 64]),
                                    r=[bv], w=[B("vS", g, tt, e)])
                        if kind == 1:
                            bw = B("q_wi", iv)
                            bk, bkb = pp.next()
                            for dc in range(DC):
                                S.op("pe", lambda i=i, dc=dc, bk=bk, tt=tt: nc.tensor.matmul(
                                    bk[:, 0:8], hh[i][:, dc, tt * 128:(tt + 1) * 128], wwi[:, dc, :],
                                    start=(dc == 0), stop=(dc == DC - 1)), r=[bh, WB], w=[bkb])
                            S.op("act", lambda iv=iv, bk=bk: nc.scalar.copy(out=wit[iv][:], in_=bk[:, 0:8]),
                                 r=[bkb], w=[bw])
                            S.dma("sp", lambda iv=iv, t0=t0: nc.sync.dma_start(out=wiS[t0:t0 + 128, :], in_=wit[iv][:]),
                                  r=[bw], w=[B("wiS", g, tt)])
            S.barrier()

        def pass_idx():
            with contextlib.ExitStack() as st:
                ki2 = sb(st, "i_ki", [128, S_LEN], BF16)
                qiz = [sb(st, "i_qiz%d" % i, [128, 2, 4, 128], BF16) for i in range(2)]
                wi = [sb(st, "i_wi%d" % i, [128, 8], F32) for i in range(2)]
                aw = [sb(st, "i_aw%d" % i, [128, 8], F32) for i in range(2)]
                sg = [sb(st, "i_sg%d" % i, [128, 8], F32) for i in range(2)]
                acc = [sb(st, "i_acc%d" % i, [128, S_LEN], F32) for i in range(2)]
                junk = sb(st, "i_junk", [128, S_LEN], BF16)
                nm = sb(st, "i_nm", [128, S_LEN], F32)
                rl = [sb(st, "i_rl%d" % i, [128, GW], F32) for i in range(6)]
                sc = [sb(st, "i_sc%d" % i, [128, 8], F32) for i in range(2)]
                nmT = [sb(st, "i_nmT%d" % i, [128, 4, 128], BF16) for i in range(2)]
                KB_ = B("i_ki")
                S.dma("sp", lambda: nc.sync.dma_start(out=ki2[:], in_=kiT[:, :]), w=[KB_])
                for i in range(2):
                    S.op("dve", lambda i=i: nc.vector.memset(qiz[i][:], 0.0), w=[B("i_qiz", i)])
                pp = psum_pool(range(8))
                r_rl = Rot([0, 1, 2, 3, 4, 5])
                r_nmT = Rot([0, 1])
                cwi = (8.0 ** -0.5) * (64.0 ** -0.5)
                junk2 = sb(st, "i_junk2", [128, S_LEN], BF16)
                ptmp = [sb(st, "i_ptmp%d" % i, [128, GW], F32) for i in range(2)]
                r_pt = Rot([0, 1])

                def cols_of(i):
                    return tuple(sc[i][:, j:j + 1] for j in range(6))

                def phase1(qt, i):
                    nk = (qt + 1) * 128
                    qc = slice(qt * 128, (qt + 1) * 128)
                    bq, bwi, bacc, bsc = B("i_qiz", i), B("i_wi", i), B("i_acc", i), B("i_sc", i)
                    for e in range(2):
                        S.dma("sp", lambda e=e: nc.sync.dma_start(
                            out=qiz[i][64 * e:64 * e + 64, e, :, :],
                            in_=qiT[:, 64 * e:64 * e + 64, qc].rearrange("c p t -> p c t")), w=[bq])
                    S.dma("sp", lambda: nc.sync.dma_start(out=wi[i][:], in_=wiS[qc, :]), w=[bwi])
                    S.op("act", lambda: nc.scalar.activation(
                        out=aw[i][:], in_=wi[i][:], func=AF.Abs, scale=cwi), r=[bwi], w=[B("i_aw", i)])
                    S.op("dve", lambda: nc.vector.tensor_scalar(
                        sg[i][:], wi[i][:], 0.0, 2.0, ALU.is_gt, ALU.mult), r=[bwi], w=[B("i_sg", i)])
                    S.op("dve", lambda: nc.vector.tensor_scalar(
                        sg[i][:], sg[i][:], -1.0, None, ALU.add), r=[B("i_sg", i)], w=[B("i_sg", i)])
                    nkc = (nk + GW - 1) // GW
                    for hh_ in range(8):
                        for kc in range(nkc):
                            k0 = kc * GW
                            k1 = min(nk, k0 + GW)
                            n = k1 - k0
                            bk, bkb = pp.next()
                            S.op("pe", lambda hh_=hh_, bk=bk, k0=k0, k1=k1, n=n: nc.tensor.matmul(
                                bk[:, 0:n], qiz[i][:, hh_ % 2, hh_ // 2, :], ki2[:, k0:k1], start=True, stop=True),
                                r=[bq, KB_], w=[bkb])
                            ir = r_rl.next()
                            br = B("i_rl", ir)
                            S.op("act", lambda hh_=hh_, bk=bk, n=n, ir=ir: nc.scalar.activation(
                                out=rl[ir][:, 0:n], in_=bk[:, 0:n], func=AF.Relu, scale=aw[i][:, hh_:hh_ + 1]),
                                r=[bkb, B("i_aw", i)], w=[br])
                            bac = B("i_acc", i, kc)
                            eng, E_ = ("dve", nc.vector)
                            if kc % 2 == 1:
                                if hh_ == 0:
                                    S.op("pool", lambda ir=ir, k0=k0, k1=k1, n=n: nc.gpsimd.tensor_scalar(
                                        acc[i][:, k0:k1], rl[ir][:, 0:n], sg[i][:, 0:1], None, ALU.mult),
                                        r=[br, B("i_sg", i)], w=[bac])
                                else:
                                    ipt = r_pt.next()
                                    bpt = B("i_ptmp", ipt)
                                    S.op("pool", lambda ir=ir, n=n, hh_=hh_, ipt=ipt: nc.gpsimd.tensor_scalar(
                                        ptmp[ipt][:, 0:n], rl[ir][:, 0:n], sg[i][:, hh_:hh_ + 1], None, ALU.mult),
                                        r=[br, B("i_sg", i)], w=[bpt])
                                    S.op("pool", lambda k0=k0, k1=k1, n=n, ipt=ipt: nc.gpsimd.tensor_tensor(
                                        acc[i][:, k0:k1], acc[i][:, k0:k1], ptmp[ipt][:, 0:n], ALU.add),
                                        r=[bpt, bac], w=[bac])
                            elif hh_ == 0:
                                S.op(eng, lambda ir=ir, k0=k0, k1=k1, n=n, E_=E_: E_.tensor_scalar(
                                    acc[i][:, k0:k1], rl[ir][:, 0:n], sg[i][:, 0:1], None, ALU.mult),
                                    r=[br, B("i_sg", i)], w=[bac])
                            else:
                                S.op(eng, lambda ir=ir, k0=k0, k1=k1, n=n, hh_=hh_, E_=E_: E_.scalar_tensor_tensor(
                                    acc[i][:, k0:k1], rl[ir][:, 0:n], sg[i][:, hh_:hh_ + 1], acc[i][:, k0:k1],
                                    ALU.mult, ALU.add), r=[br, B("i_sg", i), bac], w=[bac])
                    bacs = [B("i_acc", i, kc) for kc in range(nkc)]
                    lo, hi, mid, cnt, ge, dd = cols_of(i)
                    if nk > TOPK:
                        S.op("dve", lambda: nc.vector.tensor_reduce(hi, acc[i][:, 0:nk], AX.X, ALU.max),
                             r=bacs, w=[bsc])
                        S.op("dve", lambda: nc.vector.tensor_reduce(lo, acc[i][:, 0:nk], AX.X, ALU.min),
                             r=bacs, w=[bsc])
                        S.op("dve", lambda: nc.vector.tensor_scalar(lo, lo, -1.0, None, ALU.add), r=[bsc], w=[bsc])
                        S.op("dve", lambda: nc.vector.tensor_tensor(dd, hi, lo, ALU.subtract), r=[bsc], w=[bsc])
                    else:
                        S.op("dve", lambda: nc.vector.memset(lo, -1e29), w=[bsc])
                    S.op("dve", lambda: nc.vector.memset(acc[i][0:64, nk - 64:nk], -1e30), r=[bsc] + bacs, w=[bacc])

                def bis_t(qt, i, it):
                    lo, hi, mid, cnt, ge, dd = cols_of(i)
                    bsc = B("i_sc", i)
                    f = 2.0 ** -(it + 1)
                    S.op("dve", lambda: nc.vector.scalar_tensor_tensor(mid, dd, f, lo, ALU.mult, ALU.add),
                         r=[bsc], w=[bsc])

                def bis_count(qt, i, it, on_act):
                    nk = (qt + 1) * 128
                    lo, hi, mid, cnt, ge, dd = cols_of(i)
                    bsc, bacc = B("i_sc", i), B("i_acc", i)
                    bcnt = B("i_cnt", i)
                    if on_act:
                        S.op("act", lambda: nc.scalar.activation(
                            out=junk2[:, 0:nk], in_=acc[i][:, 0:nk], func=AF.Sign, scale=-1.0, bias=mid, accum_out=cnt),
                            r=[bacc, bsc], w=[bcnt, B("i_junk2")])
                    else:
                        S.op("dve", lambda: nc.vector.tensor_scalar(
                            junk[:, 0:nk], acc[i][:, 0:nk], mid, 0.0, ALU.is_gt, ALU.add, accum_out=cnt),
                            r=[bacc, bsc], w=[bcnt, B("i_junk")])

                def bis_upd(qt, i, it, on_act):
                    nk = (qt + 1) * 128
                    lo, hi, mid, cnt, ge, dd = cols_of(i)
                    bsc, bcnt = B("i_sc", i), B("i_cnt", i)
                    f = 2.0 ** -(it + 1)
                    if on_act:
                        S.op("dve", lambda: nc.vector.tensor_scalar(
                            ge, cnt, float(nk - 2 * TOPK) + 1.0, f, ALU.is_lt, ALU.mult), r=[bcnt], w=[bsc])
                    else:
                        S.op("dve", lambda: nc.vector.tensor_scalar(
                            ge, cnt, float(TOPK) - 0.5, f, ALU.is_gt, ALU.mult), r=[bcnt], w=[bsc])
                    S.op("dve", lambda: nc.vector.scalar_tensor_tensor(lo, dd, ge, lo, ALU.mult, ALU.add),
                         r=[bsc], w=[bsc])

                def phase3(qt, i):
                    nk = (qt + 1) * 128
                    qc = slice(qt * 128, (qt + 1) * 128)
                    lo = cols_of(i)[0]
                    bsc, bacc = B("i_sc", i), B("i_acc", i)
                    bnm = B("i_nm")
                    S.op("dve", lambda: nc.vector.tensor_scalar(
                        nm[:, 0:nk], acc[i][:, 0:nk], lo, None, ALU.is_gt), r=[bacc, bsc], w=[bnm])
                    for k4 in range((qt + 4) // 4):
                        nb = min(4, qt + 1 - k4 * 4)
                        bk, bkb = pp.next()
                        for j in range(nb):
                            kb = k4 * 4 + j
                            S.op("pe", lambda bk=bk, j=j, kb=kb: nc.tensor.transpose(
                                bk[:, j * 128:(j + 1) * 128], nm[:, kb * 128:(kb + 1) * 128], ident_f[:]),
                                r=[bnm, CB], w=[bkb])
                        it_ = r_nmT.next()
                        bt = B("i_nmT", it_)
                        S.op("act", lambda it_=it_, bk=bk, nb=nb: nc.scalar.copy(
                            out=nmT[it_][:, 0:nb, :], in_=bk[:, 0:nb * 128].rearrange("p (k t) -> p k t", t=128)),
                            r=[bkb], w=[bt])
                        S.dma("sp", lambda it_=it_, k4=k4, nb=nb: nc.sync.dma_start(
                            out=nmS[k4 * 4:k4 * 4 + nb, :, qc].rearrange("k p t -> p k t"), in_=nmT[it_][:, 0:nb, :]),
                            r=[bt], w=[B("nmS", qt, k4)])

                for p2 in range(16):
                    qa, qb = 2 * p2, 2 * p2 + 1
                    phase1(qa, 0)
                    phase1(qb, 1)
                    if (qa + 1) * 128 > TOPK:
                        for it in range(N_BISECT):
                            bis_t(qb, 1, it)
                            bis_count(qb, 1, it, True)
                            bis_t(qa, 0, it)
                            bis_count(qa, 0, it, False)
                            bis_upd(qa, 0, it, False)
                            bis_upd(qb, 1, it, True)
                    phase3(qa, 0)
                    phase3(qb, 1)
            S.barrier()

        def pass_att(L, kind, pre_issue=None):
            with contextlib.ExitStack() as st:
                kt = [sb(st, "a_k%d" % i, [128, S_LEN], BF16) for i in range(2)]
                VW = 128 if kind == 2 else 192
                vv = [sb(st, "a_v%d" % i, [128, 32, VW], BF16) for i in range(2)]
                if kind != 2:
                    Tt = [sb(st, "a_T%d" % i, [128, GW], F32) for i in range(2)]
                    ncp = [sb(st, "a_ncp%d" % i, [128, GW], F32) for i in range(2)]
                qz = [sb(st, "a_qz%d" % i, [128, 2, GW], BF16) for i in range(2)]
                pt = [sb(st, "a_p%d" % i, [128, GW], BF16) for i in range(6)] if kind != 2 else None
                numsb = [sb(st, "a_num%d" % i, [128, GW], F32) for i in range(2)] if kind != 2 else None
                ot = [sb(st, "a_o%d" % i, [128, GW], BF16) for i in range(2)]
                if kind == 0:
                    strip = sb(st, "a_strip", [128, 16, 640], BF16)
                    SB_ = B("a_strip")
                    for h4 in range(4):
                        wload(strip[:, h4 * 4:(h4 + 1) * 4, :], strips[L][:, h4 * 4:(h4 + 1) * 4, :], SB_)
                if kind == 1:
                    mk = [sb(st, "a_mk%d" % i, [128, 32, GW], BF16) for i in range(2)]
                if kind == 2:
                    et2 = sb(st, "a_e2", [128, 2 * GW], F32)
                    spt2 = [sb(st, "a_sp%d" % i, [128, 2 * GW], BF16) for i in range(3)]
                    eft2 = [sb(st, "a_ef%d" % i, [128, 2 * GW], F32) for i in range(2)]
                    pt2 = [sb(st, "a_p2%d" % i, [128, 2 * GW], BF16) for i in range(3)]
                    Rt = [sb(st, "a_R%d" % i, [128, GW], F32) for i in range(2)]
                for i in range(2):
                    S.op("dve", lambda i=i: nc.vector.memset(qz[i][:], 0.0), w=[B("a_qz", i)])
                if pre_issue is not None:
                    pre_issue()
                if kind == 2:
                    pE, pC, pA = psum_pool([0, 1, 2, 3]), psum_pool([4, 5]), psum_pool([6, 7])
                    pEp = Rot([(ppairs[0], bankB[0], bankB[1]), (ppairs[1], bankB[2], bankB[3])])
                    r_sp2, r_ef2, r_p2 = Rot([0, 1, 2]), Rot([0, 1]), Rot([0, 1, 2])
                else:
                    pS, pA = psum_pool([0, 1, 2, 3]), psum_pool([4, 5, 6, 7])
                r_p = Rot([0, 1, 2, 3, 4, 5])
                r_e, r_sp, r_ef = Rot([0, 1]), Rot([0, 1, 2, 3]), Rot([0, 1])
                pipe = []

                def _advance():
                    for ent in reversed(pipe):
                        ent[0][ent[1]]()
                        ent[1] += 1
                    pipe[:] = [ent for ent in pipe if ent[1] < len(ent[0])]

                def push(stages):
                    pipe.append([list(stages), 0])
                    _advance()

                def flush():
                    while pipe:
                        _advance()

                def load_kv(hp):
                    ik = hp % 2
                    bkt, bvv = B("a_k", ik), B("a_v", ik)
                    S.dma("sp", lambda: nc.sync.dma_start(out=kt[ik][:], in_=kT[hp, :, :]), w=[bkt])
                    for v4 in range(4):
                        if kind == 2:
                            S.dma("sp", lambda v4=v4: nc.sync.dma_start(
                                out=vv[ik][:, v4 * 8:(v4 + 1) * 8, :],
                                in_=vS[v4 * 1024:(v4 + 1) * 1024, hp * 128:(hp + 1) * 128].rearrange(
                                    "(k p) f -> p k f", p=128)), w=[bvv])
                        else:
                            S.dma("sp", lambda v4=v4: nc.sync.dma_start(
                                out=vv[ik][:, v4 * 8:(v4 + 1) * 8, :], in_=vS2[hp, :, v4 * 8:(v4 + 1) * 8, :]), w=[bvv])

                uid = 0
                load_kv(0)
                for hp in range(8):
                    ik = hp % 2
                    bkt, bvv = B("a_k", ik), B("a_v", ik)
                    for g in range(NG):
                        if g == 1 and hp + 1 < 8:
                            load_kv(hp + 1)
                        uid += 1
                        iq = uid % 2
                        cols = slice(g * GW, (g + 1) * GW)
                        bqz = B("a_qz", iq)
                        for e in range(2):
                            S.dma("sp", lambda iq=iq, e=e, hp=hp, cols=cols: nc.sync.dma_start(
                                out=qz[iq][64 * e:64 * e + 64, e, :], in_=qT[hp, 64 * e:64 * e + 64, cols]), w=[bqz])
                        bmk = None
                        if kind == 1:
                            bmk = B("a_mk", iq)
                            nkb = 4 * (g + 1)
                            S.dma("sp", lambda iq=iq, nkb=nkb, cols=cols: nc.sync.dma_start(
                                out=mk[iq][:, 0:nkb, :], in_=nmS[0:nkb, :, cols].rearrange("k p t -> p k t")), w=[bmk])
                        acc, accb = pA.next()
                        acc1 = acc1b = None
                        if kind != 2:
                            acc1, acc1b = pA.next()
                        io = uid % 2
                        bo = B("a_o", io)

                        def finalize(io=io, bo=bo, acc=acc, accb=accb, hp=hp, cols=cols, g=g):
                            S.op("act", lambda: nc.scalar.copy(out=ot[io][:], in_=acc[:]), r=[accb], w=[bo])
                            S.dma("sp", lambda: nc.sync.dma_start(out=oT[hp, :, cols], in_=ot[io][:]),
                                  r=[bo], w=[B("oT", hp, g)])

                        def fin_f1(io=io, acc0=acc, acc0b=accb, acc1=acc1, acc1b=acc1b):
                            bn, bn1, bt, bc = B("a_num", io), B("a_num1", io), B("a_T", io), B("a_ncp", io)
                            S.op("dve", lambda: nc.vector.reciprocal(numsb[io][64:128, :], acc0[64:128, :]), r=[acc0b], w=[bn])
                            if kind == 0:
                                S.op("dve", lambda: nc.vector.tensor_copy(out=ncp[io][0:64, :], in_=acc0[0:64, :]),
                                     r=[acc0b], w=[bc])
                                S.op("dve", lambda: nc.vector.reciprocal(numsb[io][0:64, :], acc1[0:64, :]), r=[acc1b], w=[bn1])
                                S.op("dve", lambda: nc.vector.tensor_copy(out=ncp[io][64:128, :], in_=acc1[64:128, :]),
                                     r=[acc1b], w=[bc])
                            else:
                                S.op("act", lambda: nc.scalar.activation(
                                    out=numsb[io][0:64, :], in_=acc1[0:64, :], func=AF.Ln), r=[acc1b], w=[bn1])
                                S.op("act", lambda: nc.scalar.activation(
                                    out=numsb[io][0:64, :], in_=numsb[io][0:64, :], func=AF.Exp, scale=-1.0), r=[bn1], w=[bn1])
                                S.op("act", lambda: nc.scalar.copy(out=ncp[io][0:64, :], in_=acc0[0:64, :]), r=[acc0b], w=[bc])
                                S.op("act", lambda: nc.scalar.copy(out=ncp[io][64:128, :], in_=acc1[64:128, :]), r=[acc1b], w=[bc])
                            S.dma("sp", lambda: nc.sync.dma_start(out=Tt[io][0:64, :], in_=numsb[io][64:128, :]), r=[bn], w=[bt])
                            S.dma("sp", lambda: nc.sync.dma_start(out=Tt[io][64:128, :], in_=numsb[io][0:64, :]), r=[bn1], w=[bt])

                        def fin_f2(io=io, bo=bo, hp=hp, cols=cols, g=g):
                            bt, bc = B("a_T", io), B("a_ncp", io)
                            S.op("dve", lambda: nc.vector.tensor_tensor(
                                ot[io][:], ncp[io][:], Tt[io][:], ALU.mult), r=[bt, bc], w=[bo])
                            S.dma("sp", lambda: nc.sync.dma_start(out=oT[hp, :, cols], in_=ot[io][:]),
                                  r=[bo], w=[B("oT", hp, g)])

                        for e in range(2):
                            h = 2 * hp + e
                            ps_ = slice(64 * e, 64 * e + 64)
                            if kind == 0:
                                tiles = []
                                for kbr in [4, 3, 2, 1, 0, 5, 6, 7]:
                                    kb = 4 * g - 4 + kbr
                                    if kb < 0:
                                        continue
                                    tiles.append((kb, max(0, 128 * (kbr - 4)), min(512, 128 * (kbr + 1)), 512 - 128 * kbr))
                            else:
                                kbs = list(range(4 * g + 3, -1, -1))
                                if kind == 1:
                                    kbs = [4 * g, 4 * g + 1, 4 * g + 2, 4 * g + 3] + list(range(4 * g - 1, -1, -1))
                                tiles = []
                                for kb in kbs:
                                    j = kb - 4 * g
                                    tiles.append((kb, 128 * j if j > 0 else 0, 512, None))
                            iR = e
                            bR = B("a_R", iR)
                            if kind == 2:
                                units = []
                                ti = 0
                                while ti < len(tiles):
                                    kb = tiles[ti][0]
                                    if kb >= 4 * g or ti + 1 >= len(tiles):
                                        units.append([ti])
                                        ti += 1
                                    else:
                                        units.append([ti, ti + 1])
                                        ti += 2
                                for un in units:
                                    tl = [dict(kb=tiles[x][0], t0=tiles[x][1], t1=tiles[x][2], n=tiles[x][2] - tiles[x][1],
                                               first=(x == 0), last=(x == len(tiles) - 1), diag=(tiles[x][0] >= 4 * g))
                                          for x in un]
                                    m = len(tl)
                                    isp, ief, ip = r_sp2.next(), r_ef2.next(), r_p2.next()
                                    bsp, bef, bp, be = B("a_sp", isp), B("a_ef", ief), B("a_p2", ip), B("a_e2")
                                    if m == 2:
                                        pfull, pb0, pb1 = pEp.next()
                                        ebks = [(pfull[:, 0:GW], pb0), (pfull[:, GW:2 * GW], pb1)]
                                    else:
                                        pfull = None
                                        ebks = [pE.next()]
                                    cbk, cbb = pC.next()
                                    W = 2 * GW if m == 2 else tl[0]["n"]
                                    lastunit = tl[-1]["last"]
                                    fin = finalize if (lastunit and e == 1) else None

                                    def stA(tl=tl, m=m, isp=isp, bsp=bsp, be=be, ebks=ebks, pfull=pfull, W=W, ik=ik, iq=iq, e=e,
                                            iR=iR, bR=bR, bkt=bkt, bqz=bqz):
                                        if tl[0]["first"]:
                                            S.op("dve", lambda: nc.vector.memset(Rt[iR][:], 0.0), w=[bR])
                                        for j, t in enumerate(tl):
                                            ebk, ebb = ebks[j]
                                            S.op("pe", lambda ebk=ebk, t=t: nc.tensor.matmul(
                                                ebk[:, t["t0"]:t["t1"]], kt[ik][:, t["kb"] * 128:(t["kb"] + 1) * 128],
                                                qz[iq][:, e, t["t0"]:t["t1"]], start=True, stop=False, skip_group_check=True),
                                                r=[bkt, bqz], w=[ebb])
                                        if m == 2:
                                            S.op("act", lambda: nc.scalar.activation(
                                                out=et2[:, 0:W], in_=pfull[:, 0:W], func=AF.Exp),
                                                r=[ebks[0][1], ebks[1][1]], w=[be])
                                        else:
                                            t = tl[0]
                                            S.op("act", lambda: nc.scalar.activation(
                                                out=et2[:, 0:W], in_=ebks[0][0][:, t["t0"]:t["t1"]], func=AF.Exp),
                                                r=[ebks[0][1]], w=[be])
                                        S.op("act", lambda: nc.scalar.activation(
                                            out=spt2[isp][:, 0:W], in_=et2[:, 0:W], func=AF.Ln, bias=1.0), r=[be], w=[bsp])
                                        if tl[0]["diag"]:
                                            S.op("dve", lambda: nc.vector.tensor_tensor(
                                                spt2[isp][:, 0:128], spt2[isp][:, 0:128], cb["tri01"][:], ALU.mult),
                                                r=[bsp, CB], w=[bsp])

                                    def stB(tl=tl, m=m, isp=isp, bsp=bsp, ief=ief, bef=bef, ip=ip, bp=bp, ebks=ebks, cbk=cbk, cbb=cbb,
                                            W=W, iR=iR, bR=bR, lastunit=lastunit):
                                        for j, t in enumerate(tl):
                                            ebk, ebb = ebks[j]
                                            n, t0, t1 = t["n"], t["t0"], t["t1"]
                                            more = t["diag"] or (m == 2 and j == 1)
                                            S.op("pe", lambda ebk=ebk, j=j, n=n, t0=t0, t1=t1, more=more: nc.tensor.matmul(
                                                ebk[:, t0:t1], cb["negu"][:], spt2[isp][:, j * GW:j * GW + n], start=False,
                                                stop=(not more), skip_group_check=True), r=[bsp, CB], w=[ebb])
                                            if m == 2 and j == 1:
                                                S.op("pe", lambda ebk=ebk: nc.tensor.matmul(
                                                    ebk[:, 0:GW], cb["negones"][:], spt2[isp][:, 0:GW], start=False, stop=True,
                                                    skip_group_check=True), r=[bsp, CB], w=[ebb])
                                            if t["diag"]:
                                                S.op("pe", lambda ebk=ebk, t0=t0: nc.tensor.matmul(
                                                    ebk[:, t0:t0 + 128], ident_b[:], cb["negtri"][:], start=False, stop=True,
                                                    skip_group_check=True), r=[CB], w=[ebb])
                                        t0, t1 = tl[0]["t0"], tl[0]["t1"]
                                        if not lastunit:
                                            for j, t in enumerate(tl):
                                                S.op("pe", lambda j=j, t=t: nc.tensor.matmul(
                                                    cbk[:, t0:t1], cb["negones"][:], spt2[isp][:, j * GW:j * GW + t["n"]],
                                                    start=(j == 0), stop=(j == m - 1)), r=[bsp, CB], w=[cbb])
                                        for j, t in enumerate(tl):
                                            ebk, ebb = ebks[j]
                                            S.op("dve", lambda ebk=ebk, j=j, t=t: nc.vector.tensor_tensor(
                                                eft2[ief][:, j * GW:j * GW + t["n"]], ebk[:, t["t0"]:t["t1"]],
                                                Rt[iR][:, t["t0"]:t["t1"]], ALU.add), r=[ebb, bR], w=[bef])
                                        S.op("act", lambda: nc.scalar.activation(
                                            out=pt2[ip][:, 0:W], in_=eft2[ief][:, 0:W], func=AF.Exp), r=[bef], w=[bp])
                                        if not lastunit:
                                            S.op("dve", lambda: nc.vector.tensor_tensor(
                                                Rt[iR][:, t0:t1], cbk[:, t0:t1], Rt[iR][:, t0:t1], ALU.add), r=[cbb, bR], w=[bR])

                                    def stC(tl=tl, ip=ip, bp=bp, acc=acc, accb=accb, ps_=ps_, ik=ik, e=e, bvv=bvv, fin=fin):
                                        for j, t in enumerate(tl):
                                            S.op("pe", lambda j=j, t=t: nc.tensor.matmul(
                                                acc[ps_, t["t0"]:t["t1"]], vv[ik][:, t["kb"], 64 * e:64 * e + 64],
                                                pt2[ip][:, j * GW:j * GW + t["n"]], start=t["first"], stop=t["last"],
                                                skip_group_check=True), r=[bvv, bp], w=[accb])
                                        if fin is not None:
                                            fin()

                                    push([stA, stB, stC])
                                continue
                            for ti, (kb, t0, t1, u0) in enumerate(tiles):
                                n = t1 - t0
                                first = (ti == 0)
                                last = (ti == len(tiles) - 1)
                                fin = finalize if (last and e == 1) else None
                                kcol = slice(kb * 128, (kb + 1) * 128)
                                ip = r_p.next()
                                bp = B("a_p", ip)
                                if kind in (0, 1):
                                    sbk, sbb = pS.next()

                                    def stA(sbk=sbk, sbb=sbb, ik=ik, kcol=kcol, iq=iq, e=e, t0=t0, t1=t1, n=n, h=h, u0=u0,
                                            kb=kb, ip=ip, bp=bp, bkt=bkt, bqz=bqz, bmk=bmk):
                                        S.op("pe", lambda: nc.tensor.matmul(
                                            sbk[:, t0:t1], kt[ik][:, kcol], qz[iq][:, e, t0:t1], start=True, stop=(kind == 1)),
                                            r=[bkt, bqz], w=[sbb])
                                        if kind == 0:
                                            S.op("pe", lambda: nc.tensor.matmul(
                                                sbk[:, t0:t1], ident_b[:], strip[:, h, u0 + t0:u0 + t1], start=False, stop=True),
                                                r=[SB_, CB], w=[sbb])
                                        S.op("act", lambda: nc.scalar.activation(
                                            out=pt[ip][:, 0:n], in_=sbk[:, t0:t1], func=AF.Exp), r=[sbb], w=[bp])

                                    def stB(ip=ip, bp=bp, n=n, iq=iq, kb=kb, t0=t0, t1=t1, bmk=bmk):
                                        S.op("dve", lambda: nc.vector.tensor_tensor(
                                            pt[ip][:, 0:n], pt[ip][:, 0:n], mk[iq][:, kb, t0:t1], ALU.mult),
                                            r=[bp, bmk], w=[bp])

                                    acc_e, acc_eb = (acc, accb) if e == 0 else (acc1, acc1b)
                                    f1 = fin_f1 if (last and e == 1) else None
                                    f2 = fin_f2 if (last and e == 1) else None

                                    def stC(acc_e=acc_e, acc_eb=acc_eb, ik=ik, kb=kb, e=e, ip=ip, bp=bp,
                                            t0=t0, t1=t1, n=n, first=first, last=last, bvv=bvv, f1=f1):
                                        S.op("pe", lambda: nc.tensor.matmul(
                                            acc_e[:, t0:t1], vv[ik][:, kb, 64 * e:64 * e + 128], pt[ip][:, 0:n],
                                            start=first, stop=last), r=[bvv, bp], w=[acc_eb])
                                        if f1 is not None:
                                            f1()

                                    stages = [stA, stB, stC] if kind == 1 else [stA, stC]
                                    if f2 is not None:
                                        stages = stages + [(lambda: None)] * 4 + [f2]
                                    push(stages)
                flush()
            S.barrier()

        def pass_wo(L):
            gcol = SM_N2 + 8 * L
            with contextlib.ExitStack() as st:
                wt = sb(st, "o_w", [128, DC, D], BF16)
                WB = B("o_w")
                for dc in range(DC):
                    wload(wt[:, dc, :], w_o[L][dc * 128:(dc + 1) * 128, :], WB)
                oo = [sb(st, "o_o%d" % i, [128, DC, GW], BF16) for i in range(2)]
                xt = [sb(st, "o_x%d" % i, [128, DC, GW], F32) for i in range(1)]
                sq = sb(st, "o_sq", [128, DC, GW], BF16)
                hh = sb(st, "o_h", [128, DC, GW], BF16)
                rs = sb(st, "o_rs", [128, GW], F32)
                pp = psum_pool(range(8))
                bsq, bh, brs = B("o_sq"), B("o_h"), B("o_rs")
                def load_o(g):
                    i = g % 2
                    cols = slice(g * GW, (g + 1) * GW)
                    S.dma("sp", lambda: nc.sync.dma_start(
                        out=oo[i][:], in_=oT[:, :, cols].rearrange("c p t -> p c t")), w=[B("o_o", i)])

                load_o(0)
                for g in range(NG):
                    i = g % 2
                    cols = slice(g * GW, (g + 1) * GW)
                    bo, bx = B("o_o", i), B("o_x", 0)
                    if g + 1 < NG:
                        load_o(g + 1)
                    S.dma("sp", lambda cols=cols: nc.sync.dma_start(
                        out=xt[0][:], in_=xT[:, :, cols].rearrange("c p t -> p c t")), r=[B("xT", g)], w=[bx])
                    for dch in range(DC):
                        bk, bkb = pp.next()
                        for c in range(DC):
                            S.op("pe", lambda i=i, c=c, dch=dch, bk=bk: nc.tensor.matmul(
                                bk[:], wt[:, c, dch * 128:(dch + 1) * 128], oo[i][:, c, :],
                                start=(c == 0), stop=(c == DC - 1)), r=[bo, WB], w=[bkb])
                        S.op("dve", lambda dch=dch, bk=bk: nc.vector.tensor_tensor(
                            xt[0][:, dch, :], bk[:], xt[0][:, dch, :], ALU.add), r=[bkb, bx], w=[bx])
                    S.dma("sp", lambda cols=cols: nc.sync.dma_start(
                        out=xT[:, :, cols].rearrange("c p t -> p c t"), in_=xt[0][:]), r=[bx], w=[B("xT", g)])
                    S.op("act", lambda: nc.scalar.activation(out=sq[:], in_=xt[0][:], func=AF.Square), r=[bx], w=[bsq])
                    bk, bkb = pp.next()
                    for dc in range(DC):
                        S.op("pe", lambda dc=dc, bk=bk: nc.tensor.matmul(
                            bk[:], cb["ones"][:], sq[:, dc, :], start=(dc == 0), stop=(dc == DC - 1)),
                            r=[bsq, CB], w=[bkb])
                    S.op("act", lambda bk=bk: nc.scalar.activation(
                        out=rs[:], in_=bk[:], func=AF.Ln, bias=epsb[:, 0:1], scale=1.0 / D), r=[bkb, CB], w=[brs])
                    S.op("act", lambda: nc.scalar.activation(out=rs[:], in_=rs[:], func=AF.Exp, scale=-0.5),
                         r=[brs], w=[brs])
                    for dc in range(DC):
                        S.op("dve", lambda dc=dc: nc.vector.scalar_tensor_tensor(
                            hh[:, dc, :], xt[0][:, dc, :], smalls[:, gcol + dc:gcol + dc + 1], rs[:],
                            ALU.mult, ALU.mult), r=[bx, brs, CB], w=[bh])
                    S.dma("sp", lambda cols=cols: nc.sync.dma_start(
                        out=hT[:, :, cols].rearrange("c p t -> p c t"), in_=hh[:]), r=[bh], w=[B("hT", g)])
            S.barrier()

        def alloc_ffn_w(st):
            w1 = sb(st, "f_w1", [128, DC, 2 * DFF], BF16)
            w2 = sb(st, "f_w2", [128, NJ, D], BF16)
            return w1, w2

        def issue_ffn_w2(L, w2):
            W2B = B("f_w2")
            for j2 in range(0, NJ, 2):
                wload(w2[:, j2:j2 + 2, :], w_dn[L][j2 * 128:(j2 + 2) * 128, :].rearrange("(j p) d -> p j d", p=128), W2B)

        def issue_ffn_w(L, w1, w2, skip_w2=False):
            W1B = B("f_w1")
            for q4 in range(4):
                c0, c1 = q4 * 704, (q4 + 1) * 704
                for hf in range(2):
                    for dc in range(DC):
                        wload(w1[:, dc, hf * DFF + c0:hf * DFF + c1],
                              w_in[L][dc * 128:(dc + 1) * 128, hf * DFF + c0:hf * DFF + c1], W1B)
            if not skip_w2:
                issue_ffn_w2(L, w2)

        def pass_ffn(L, final, w1, w2):
            W1B, W2B = B("f_w1"), B("f_w2")
            with contextlib.ExitStack() as st:
                hh = [sb(st, "f_h%d" % i, [128, DC, GW], BF16) for i in range(2)]
                xc = [sb(st, "f_xc%d" % i, [128, GW], F32) for i in range(2)]
                mm = sb(st, "f_m", [128, NJ, GW], BF16)
                ab = [sb(st, "f_ab%d" % i, [128, GW + 2], F32) for i in range(3)]
                t1 = [sb(st, "f_t1%d" % i, [128, GW], F32) for i in range(2)]
                t2 = [sb(st, "f_t2%d" % i, [128, GW], F32) for i in range(2)]
                cg = [sb(st, "f_cg%d" % i, [128, GW], F32) for i in range(2)]
                cu = [sb(st, "f_cu%d" % i, [128, GW], F32) for i in range(2)]
                hs = sb(st, "f_hs", [128, NFC, 2], F32)
                yt = sb(st, "f_y", [128, 4, 128], F32) if final else None
                S.op("dve", lambda: nc.vector.memset(hs[:], 0.0), w=[B("f_hs", fc) for fc in range(NFC)])
                pp = psum_pool(range(8))
                r_ab, r_t1, r_t2, r_xc = Rot([0, 1, 2]), Rot([0, 1]), Rot([0, 1]), Rot([0, 1])
                cw = lambda tap, fc: smalls[:, SM_CW + (L * 3 + tap) * 44 + fc:SM_CW + (L * 3 + tap) * 44 + fc + 1]
                cbias = lambda fc: smalls[:, SM_CB + L * 44 + fc:SM_CB + L * 44 + fc + 1]
                def load_h(g):
                    i = g % 2
                    cols = slice(g * GW, (g + 1) * GW)
                    S.dma("sp", lambda: nc.sync.dma_start(
                        out=hh[i][:], in_=hT[:, :, cols].rearrange("c p t -> p c t")), w=[B("f_h", i)])

                load_h(0)
                for g in range(NG):
                    i = g % 2
                    cols = slice(g * GW, (g + 1) * GW)
                    bh = B("f_h", i)
                    if g + 1 < NG:
                        load_h(g + 1)
                    for j in range(NJ):
                        ipair = (g * NJ + j) % 2
                        for part, fc in ((0, j), (1, j + NJ)):
                            bk, bkb = pp.next()
                            for dc in range(DC):
                                S.op("pe", lambda i=i, dc=dc, bk=bk, fc=fc: nc.tensor.matmul(
                                    bk[:], w1[:, dc, fc * 128:(fc + 1) * 128], hh[i][:, dc, :],
                                    start=(dc == 0), stop=(dc == DC - 1)), r=[bh, W1B], w=[bkb])
                            ia, i1, i2 = r_ab.next(), r_t1.next(), r_t2.next()
                            ba, b1, b2, bhs = B("f_ab", ia), B("f_t1", i1), B("f_t2", i2), B("f_hs", fc)
                            S.op("pool", lambda ia=ia, fc=fc: nc.gpsimd.tensor_copy(out=ab[ia][:, 0:2], in_=hs[:, fc, :]),
                                 r=[bhs], w=[ba])
                            S.op("act", lambda ia=ia, bk=bk: nc.scalar.copy(out=ab[ia][:, 2:GW + 2], in_=bk[:]),
                                 r=[bkb], w=[ba])
                            S.op("pool", lambda ia=ia, fc=fc: nc.gpsimd.tensor_copy(out=hs[:, fc, :], in_=ab[ia][:, GW:GW + 2]),
                                 r=[ba], w=[bhs])
                            S.op("act", lambda ia=ia, i1=i1, fc=fc: nc.scalar.activation(
                                out=t1[i1][:], in_=ab[ia][:, 0:GW], func=AF.Identity, bias=cbias(fc), scale=cw(0, fc)),
                                r=[ba, CB], w=[b1])
                            S.op("dve", lambda ia=ia, i1=i1, i2=i2, fc=fc: nc.vector.scalar_tensor_tensor(
                                t2[i2][:], ab[ia][:, 1:GW + 1], cw(1, fc), t1[i1][:], ALU.mult, ALU.add),
                                r=[ba, b1, CB], w=[b2])
                            dst, bdst = (cg[ipair], B("f_cg", ipair)) if part == 0 else (cu[ipair], B("f_cu", ipair))
                            S.op("dve", lambda ia=ia, i2=i2, fc=fc, dst=dst: nc.vector.scalar_tensor_tensor(
                                dst[:], ab[ia][:, 2:GW + 2], cw(2, fc), t2[i2][:], ALU.mult, ALU.add),
                                r=[ba, b2, CB], w=[bdst])
                        bcg = B("f_cg", ipair)
                        S.op("act", lambda ipair=ipair: nc.scalar.activation(
                            out=cg[ipair][:], in_=cg[ipair][:], func=AF.Silu), r=[bcg], w=[bcg])
                        S.op("dve", lambda ipair=ipair, j=j: nc.vector.tensor_tensor(
                            mm[:, j, :], cg[ipair][:], cu[ipair][:], ALU.mult), r=[bcg, B("f_cu", ipair)], w=[B("f_m", j)])
                    for dch in range(DC):
                        ix = r_xc.next()
                        bx = B("f_xc", ix)
                        S.dma("sp", lambda ix=ix, dch=dch, cols=cols: nc.sync.dma_start(
                            out=xc[ix][:], in_=xT[dch, :, cols]), r=[B("xT", dch, g)], w=[bx])
                        bk, bkb = pp.next()
                        for j in range(NJ):
                            S.op("pe", lambda j=j, dch=dch, bk=bk: nc.tensor.matmul(
                                bk[:], w2[:, j, dch * 128:(dch + 1) * 128], mm[:, j, :],
                                start=(j == 0), stop=(j == NJ - 1)), r=[B("f_m", j), W2B], w=[bkb])
                        S.op("dve", lambda ix=ix, bk=bk: nc.vector.tensor_tensor(
                            xc[ix][:], bk[:], xc[ix][:], ALU.add), r=[bkb, bx], w=[bx])
                        if not final:
                            S.dma("sp", lambda ix=ix, dch=dch, cols=cols: nc.sync.dma_start(
                                out=xT[dch, :, cols], in_=xc[ix][:]), r=[bx], w=[B("xT", dch, g)])
                        else:
                            bk2, bkb2 = pp.next()
                            for tt in range(4):
                                S.op("pe", lambda bk2=bk2, ix=ix, tt=tt: nc.tensor.transpose(
                                    bk2[:, tt * 128:(tt + 1) * 128], xc[ix][:, tt * 128:(tt + 1) * 128], ident_f[:]),
                                    r=[bx, CB], w=[bkb2])
                            by = B("f_y")
                            S.op("act", lambda bk2=bk2: nc.scalar.copy(
                                out=yt[:], in_=bk2[:].rearrange("p (t d) -> p t d", d=128)), r=[bkb2], w=[by])
                            S.dma("sp", lambda g=g, dch=dch: nc.sync.dma_start(
                                out=y_out[g * GW:(g + 1) * GW, dch * 128:(dch + 1) * 128].rearrange("(t p) d -> p t d", p=128),
                                in_=yt[:]), r=[by], w=[B("y", g, dch)])
            S.barrier()

        for L in range(n_layers):
            kind = L % 3
            with contextlib.ExitStack() as wst:
                wt, wwi = load_qkv_w(wst, L, kind)
                pass_norm(SM_N1 + 8 * L, first=(L == 0))
                pass_qkv(L, kind, wt, wwi)
            if kind == 1:
                pass_idx()
            with contextlib.ExitStack() as wst:
                if kind == 2:
                    w1, w2 = alloc_ffn_w(wst)
                    pass_att(L, kind, pre_issue=lambda: issue_ffn_w(L, w1, w2))
                elif kind == 0:
                    w1 = sb(wst, "f_w1", [128, DC, 2 * DFF], BF16)
                    pass_att(L, kind, pre_issue=lambda: issue_ffn_w(L, w1, None, skip_w2=True))
                    w2 = sb(wst, "f_w2", [128, NJ, D], BF16)
                    issue_ffn_w2(L, w2)
                else:
                    w2 = sb(wst, "f_w2", [128, NJ, D], BF16)
                    pass_att(L, kind, pre_issue=lambda: issue_ffn_w2(L, w2))
                    w1 = sb(wst, "f_w1", [128, DC, 2 * DFF], BF16)
                    issue_ffn_w(L, w1, w2, skip_w2=True)
                pass_wo(L)
                pass_ffn(L, (L == n_layers - 1), w1, w2)
        S.emit()
    return nc


_CACHE = {}


def make_in_maps(inputs, n_layers=4):
    c = host_consts()
    idx, band = strip_index()
    sm = pack_smalls(inputs)
    shared = {"smalls": sm, "rcos": c["rcos"], "rsin": c["rsin"]}
    for n in CONST_NAMES:
        shared["c_" + n] = c[n]
    f = lambda a: np.ascontiguousarray(np.asarray(a, dtype=np.float32))
    shared["wqkv0"] = f(inputs["a_w_qkv"][0])
    shared["wqkv1"] = f(inputs["b_w_in"][0])
    shared["wqkv2"] = f(inputs["c_w_qkv"][0])
    shared["wqkv3"] = f(inputs["a_w_qkv"][1])
    shared["wo0"] = f(inputs["a_w_o"][0])
    shared["wo1"] = f(inputs["b_w_o"][0])
    shared["wo2"] = f(inputs["c_w_o"][0])
    shared["wo3"] = f(inputs["a_w_o"][1])
    for L in range(4):
        shared["win%d" % L] = f(inputs["ffn_w_in"][L])
        shared["wdn%d" % L] = f(inputs["ffn_w_down"][L])
    for L, ia in ((0, 0), (3, 1)):
        rb = np.asarray(inputs["a_rel_bias"][ia], np.float32)
        st = rb[:, idx]
        st = np.where(band[None], st, np.float32(NEG))
        shared["strip%d" % L] = np.ascontiguousarray(st.transpose(1, 0, 2))
    x = np.asarray(inputs["x"], np.float32)
    maps = []
    for b in range(8):
        m = dict(shared)
        m["x"] = np.ascontiguousarray(x[b])
        maps.append(m)
    return maps


def kernel(**inputs):
    inputs = {k: np.asarray(v) for k, v in inputs.items()}
    if "nc" not in _CACHE:
        _CACHE["nc"] = build(4)
    nc = _CACHE["nc"]
    maps = make_in_maps(inputs)
    res = run_bass_kernel_spmd(nc, maps, core_ids=list(range(8)))
    out = np.stack([np.asarray(r["y"], np.float32) for r in res.results], axis=0)
    return out
```

```python
import contextlib
import numpy as np
import concourse.bass as bass
import concourse.mybir as mybir
from concourse.bass_utils import run_bass_kernel_spmd

F32 = mybir.dt.float32
BF16 = mybir.dt.bfloat16
AF = mybir.ActivationFunctionType
ALU = mybir.AluOpType
AX = mybir.AxisListType

S_LEN = 4096
D = 1024
DC = 8
NG = 8
GW = 512
DFF = 2816
NFC = 44
NJ = 22
EPS = 1e-6
NEG = -30000.0
N_BISECT = 16
TOPK = 256


class Buf:
    __slots__ = ("name", "w", "r")

    def __init__(self, name):
        self.name = name
        self.w = None
        self.r = []


class Op:
    __slots__ = ("eng", "fn", "deps", "is_dma", "signal", "token", "slot", "idx")


class Sched:
    ENGS = ("pe", "act", "dve", "pool", "sp")
    KSLOT = 8

    def __init__(self, nc, es):
        self.nc = nc
        self.e = {"pe": nc.tensor, "act": nc.scalar, "dve": nc.vector, "pool": nc.gpsimd, "sp": nc.sync}
        self.ops = []
        self.last = {e: None for e in self.ENGS}
        self.dma_since = []
        self.dma_hist = {"sp": [], "pool": []}
        self.bufs = {}
        self.sem = {e: es.enter_context(nc.semaphore("s_" + e)) for e in self.ENGS}
        self.dsem = {q: [es.enter_context(nc.semaphore("d_%s%d" % (q, i))) for i in range(self.KSLOT)]
                     for q in ("sp", "pool")}

    def B(self, *key):
        b = self.bufs.get(key)
        if b is None:
            b = Buf(key)
            self.bufs[key] = b
        return b

    def _add(self, eng, fn, r, w, is_dma):
        op = Op()
        op.eng = eng
        op.fn = fn
        op.is_dma = is_dma
        op.signal = is_dma
        op.token = None
        op.slot = None
        op.idx = len(self.ops)
        deps = set()
        for b in list(r) + list(w):
            if b.w is not None:
                deps.add(b.w)
        for b in w:
            for x in b.r:
                deps.add(x)
        real = set()
        for d in deps:
            dop = self.ops[d]
            if dop.is_dma or is_dma:
                real.add(d)
            elif dop.eng != eng:
                real.add(d)
            else:
                if eng != "pe" and any(b.w == d for b in r):
                    real.add(d)
        if is_dma:
            h = self.dma_hist[eng]
            if len(h) >= self.KSLOT:
                real.add(h[-self.KSLOT])
            op.slot = len(h) % self.KSLOT
            h.append(op.idx)
            self.dma_since.append(op.idx)
        op.deps = real
        for d in real:
            self.ops[d].signal = True
        self.ops.append(op)
        for b in w:
            b.w = op.idx
            b.r = []
        for b in r:
            if b not in w:
                b.r.append(op.idx)
        if not is_dma:
            self.last[eng] = op.idx
        return op

    def op(self, eng, fn, r=(), w=()):
        return self._add(eng, fn, r, w, False)

    def dma(self, q, fn, r=(), w=()):
        return self._add(q, fn, r, w, True)

    def barrier(self):
        lasts = [v for v in self.last.values() if v is not None]
        dmas = list(self.dma_since)
        for eng in self.ENGS:
            op = Op()
            op.eng = eng
            op.fn = None
            op.is_dma = False
            op.signal = False
            op.token = None
            op.slot = None
            op.idx = len(self.ops)
            op.deps = set(d for d in lasts if self.ops[d].eng != eng) | set(dmas)
            for d in op.deps:
                self.ops[d].signal = True
            self.ops.append(op)
        self.dma_since = []
        for b in self.bufs.values():
            b.w = None
            b.r = []

    def emit(self):
        cnt = {e: 0 for e in self.ENGS}
        dcnt = {q: [0] * self.KSLOT for q in ("sp", "pool")}
        seen = {e: {} for e in self.ENGS}
        for op in self.ops:
            eng = self.e[op.eng]
            need = {}
            for d in op.deps:
                sem, val = self.ops[d].token
                k = id(sem)
                if k not in need or need[k][1] < val:
                    need[k] = (sem, val)
            sn = seen[op.eng]
            for k, (sem, val) in need.items():
                if sn.get(k, 0) >= val:
                    continue
                sn[k] = val
                eng.wait_ge(sem, val)
            if op.fn is None:
                continue
            ins = op.fn()
            if op.is_dma:
                dcnt[op.eng][op.slot] += 16
                sem = self.dsem[op.eng][op.slot]
                ins.then_inc(sem, 16)
                op.token = (sem, dcnt[op.eng][op.slot])
            elif op.signal:
                cnt[op.eng] += 1
                ins.then_inc(self.sem[op.eng], 1)
                op.token = (self.sem[op.eng], cnt[op.eng])
        for q in ("sp", "pool"):
            for i in range(self.KSLOT):
                if dcnt[q][i]:
                    self.e[q].wait_ge(self.dsem[q][i], dcnt[q][i])


class Rot:
    def __init__(self, items):
        self.items = items
        self.i = 0

    def next(self):
        it = self.items[self.i % len(self.items)]
        self.i += 1
        return it


def host_consts():
    c = {}
    c["ident_f"] = np.eye(128, dtype=np.float32)
    c["ones"] = np.ones((128, 128), np.float32)
    bo = np.zeros((128, 128), np.float32)
    bo[:64, :64] = 1
    bo[64:, 64:] = 1
    c["blockones"] = bo
    pm = np.zeros((128, 128), np.float32)
    for m in range(128):
        if m % 64 < 32:
            pm[m + 32, m] = -1.0
        else:
            pm[m - 32, m] = 1.0
    c["pmat"] = pm
    j = np.arange(128)[:, None]
    s = np.arange(128)[None, :]
    c["negu"] = -(j >= s).astype(np.float32)
    c["negones"] = -np.ones((128, 128), np.float32)
    ss = np.arange(128)[:, None]
    tt = np.arange(128)[None, :]
    c["tri01"] = (tt > ss).astype(np.float32)
    c["negtri"] = np.where(tt > ss, 0.0, NEG).astype(np.float32)
    inv = 10000.0 ** (-np.arange(0, 64, 2, dtype=np.float32) / 64)
    ang = np.arange(S_LEN, dtype=np.float32)[:, None] * inv[None, :]
    cs = np.cos(ang).astype(np.float32).T
    sn = np.sin(ang).astype(np.float32).T
    c["rcos"] = np.ascontiguousarray(np.tile(cs, (4, 1)))
    c["rsin"] = np.ascontiguousarray(np.tile(sn, (4, 1)))
    return c


def strip_index():
    s = np.arange(128)[:, None]
    u = np.arange(640)[None, :]
    idx = np.clip(u - s, -256, 256) + 256
    band = (u // 64 - 8 <= s // 64) & (s // 64 <= u // 64)
    return idx, band


CONST_NAMES = ["ident_f", "ones", "blockones", "pmat", "negu", "negones", "tri01", "negtri"]
SM_N1 = 0
SM_N2 = 32
SM_CW = 64
SM_CB = SM_CW + 4 * 3 * 44
SM_QK = SM_CB + 4 * 44
SM_NS = SM_QK + 8


def pack_smalls(inp):
    sm = np.zeros((128, SM_NS), np.float32)
    for L in range(4):
        sm[:, SM_N1 + 8 * L:SM_N1 + 8 * L + 8] = inp["norm1_g"][L].reshape(8, 128).T
        sm[:, SM_N2 + 8 * L:SM_N2 + 8 * L + 8] = inp["norm2_g"][L].reshape(8, 128).T
        for i in range(3):
            o = SM_CW + (L * 3 + i) * 44
            sm[:, o:o + 44] = inp["ffn_conv_w"][L, i].reshape(44, 128).T
        o = SM_CB + L * 44
        sm[:, o:o + 44] = inp["ffn_conv_b"][L].reshape(44, 128).T
    qk = {0: (inp["a_q_norm"][0], inp["a_k_norm"][0]), 1: (inp["b_q_norm"][0], inp["b_k_norm"][0]),
          3: (inp["a_q_norm"][1], inp["a_k_norm"][1])}
    for L, (qg, kg) in qk.items():
        sm[:, SM_QK + 2 * L] = np.tile(qg, 2)
        sm[:, SM_QK + 2 * L + 1] = np.tile(kg, 2)
    return sm


def build(n_layers=4):
    nc = bass.Bass("TRN2", target_bir_lowering=False)
    dt = nc.dram_tensor
    x_in = dt("x", [S_LEN, D], F32, kind="ExternalInput").ap()
    y_out = dt("y", [S_LEN, D], F32, kind="ExternalOutput").ap()
    w_qkv = {0: dt("wqkv0", [D, 3 * D], F32, kind="ExternalInput").ap(),
             1: dt("wqkv1", [D, 3656], F32, kind="ExternalInput").ap(),
             2: dt("wqkv2", [D, 3 * D], F32, kind="ExternalInput").ap(),
             3: dt("wqkv3", [D, 3 * D], F32, kind="ExternalInput").ap()}
    w_o = {L: dt("wo%d" % L, [D, D], F32, kind="ExternalInput").ap() for L in range(4)}
    w_in = {L: dt("win%d" % L, [D, 2 * DFF], F32, kind="ExternalInput").ap() for L in range(4)}
    w_dn = {L: dt("wdn%d" % L, [DFF, D], F32, kind="ExternalInput").ap() for L in range(4)}
    strips = {L: dt("strip%d" % L, [128, 16, 640], F32, kind="ExternalInput").ap() for L in (0, 3)}
    smalls_d = dt("smalls", [128, SM_NS], F32, kind="ExternalInput").ap()
    consts_d = {n: dt("c_" + n, [128, 128], F32, kind="ExternalInput").ap() for n in CONST_NAMES}
    rcos_d = dt("rcos", [128, S_LEN], F32, kind="ExternalInput").ap()
    rsin_d = dt("rsin", [128, S_LEN], F32, kind="ExternalInput").ap()
    xT = dt("xT", [DC, 128, S_LEN], F32).ap()
    hT = dt("hT", [DC, 128, S_LEN], BF16).ap()
    qT = dt("qT", [DC, 128, S_LEN], BF16).ap()
    kT = dt("kT", [DC, 128, S_LEN], BF16).ap()
    vS = dt("vS", [S_LEN, D], BF16).ap()
    oT = dt("oT", [DC, 128, S_LEN], BF16).ap()
    vS2 = dt("vS2", [8, 128, 32, 192], BF16).ap()
    qiT = dt("qiT", [4, 128, S_LEN], BF16).ap()
    kiT = dt("kiT", [128, S_LEN], BF16).ap()
    wiS = dt("wiS", [S_LEN, 8], F32).ap()
    nmS = dt("nmS", [32, 128, S_LEN], BF16).ap()

    with contextlib.ExitStack() as es:
        S = Sched(nc, es)
        B = S.B

        _uniq = [0]

        def sb(st, name, shape, dtype):
            _uniq[0] += 1
            return st.enter_context(nc.sbuf_tensor("sb%d_%s" % (_uniq[0], name), shape, dtype))

        smalls = sb(es, "smalls", [128, SM_NS], F32)
        cb = {n: sb(es, "cb_" + n, [128, 128], BF16) for n in CONST_NAMES if n != "ident_f"}
        ident_f = sb(es, "ident_f", [128, 128], F32)
        tri01_f = sb(es, "tri01_f", [128, 128], F32)
        epsb = sb(es, "epsb", [128, 1], F32)
        ppairs = [es.enter_context(nc.psum_tensor("ppair%d" % i, [128, 1024], F32)) for i in range(4)]
        banks = [ppairs[i // 2][:, (i % 2) * 512:(i % 2 + 1) * 512] for i in range(8)]
        bankB = [B("bank", i) for i in range(8)]
        CB = B("consts")
        S.dma("sp", lambda: nc.sync.dma_start(out=smalls[:], in_=smalls_d[:, :]), w=[CB])
        S.dma("sp", lambda: nc.sync.dma_start(out=ident_f[:], in_=consts_d["ident_f"][:, :]), w=[CB])
        S.dma("sp", lambda: nc.sync.dma_start(out=tri01_f[:], in_=consts_d["tri01"][:, :]), w=[CB])
        for n in CONST_NAMES:
            if n == "ident_f":
                continue
            S.dma("pool", (lambda n=n: nc.gpsimd.dma_start(out=cb[n][:], in_=consts_d[n][:, :])), w=[CB])
        S.op("dve", lambda: nc.vector.memset(epsb[:], EPS), w=[CB])
        ident_b = sb(es, "ident_b", [128, 128], BF16)
        S.dma("pool", lambda: nc.gpsimd.dma_start(out=ident_b[:], in_=consts_d["ident_f"][:, :]), w=[CB])
        S.barrier()
        with contextlib.ExitStack() as ist:
            ones_blk = sb(ist, "ones_blk", [128, 32, 64], BF16)
            S.op("dve", lambda: nc.vector.memset(ones_blk[:], 1.0), w=[B("ones_blk")])
            for hp_ in range(8):
                S.dma("sp", lambda hp_=hp_: nc.sync.dma_start(out=vS2[hp_, :, :, 64:128], in_=ones_blk[:]),
                      r=[B("ones_blk")], w=[B("vS2ones", hp_)])
            S.barrier()

        def psum_pool(ids):
            return Rot([(banks[i], bankB[i]) for i in ids])

        def wload(dst_ap, src_ap, wb):
            S.dma("pool", lambda: nc.gpsimd.dma_start(out=dst_ap, in_=src_ap, max_dma_last_dim=4096), w=[wb])

        def pass_norm(gcol, first):
            with contextlib.ExitStack() as st:
                xt = [sb(st, "n_x%d" % i, [128, DC, GW], F32) for i in range(2)]
                sq = [sb(st, "n_sq%d" % i, [128, DC, GW], BF16) for i in range(2)]
                hh = [sb(st, "n_h%d" % i, [128, DC, GW], BF16) for i in range(2)]
                rs = [sb(st, "n_rs%d" % i, [128, GW], F32) for i in range(2)]
                xin = [sb(st, "n_xin%d" % i, [128, 4, D], F32) for i in range(2)] if first else None
                pp = psum_pool(range(8))
                for g in range(NG):
                    i = g % 2
                    cols = slice(g * GW, (g + 1) * GW)
                    bx, bsq, bh, brs = B("n_x", i), B("n_sq", i), B("n_h", i), B("n_rs", i)
                    if first:
                        bxi = B("n_xin", i)
                        S.dma("sp", lambda i=i, g=g: nc.sync.dma_start(
                            out=xin[i][:], in_=x_in[g * GW:(g + 1) * GW, :].rearrange("(t p) d -> p t d", p=128)),
                            w=[bxi])
                        for dc in range(DC):
                            bk, bkb = pp.next()
                            for tt in range(4):
                                S.op("pe", lambda i=i, dc=dc, tt=tt, bk=bk: nc.tensor.transpose(
                                    bk[:, tt * 128:(tt + 1) * 128], xin[i][:, tt, dc * 128:(dc + 1) * 128],
                                    ident_f[:]), r=[bxi, CB], w=[bkb])
                            S.op("act", lambda i=i, dc=dc, bk=bk: nc.scalar.copy(out=xt[i][:, dc, :], in_=bk[:]),
                                 r=[bkb], w=[bx])
                        S.dma("sp", lambda i=i, cols=cols: nc.sync.dma_start(
                            out=xT[:, :, cols].rearrange("c p t -> p c t"), in_=xt[i][:]), r=[bx], w=[B("xT", g)])
                    else:
                        S.dma("sp", lambda i=i, cols=cols: nc.sync.dma_start(
                            out=xt[i][:], in_=xT[:, :, cols].rearrange("c p t -> p c t")), w=[bx])
                    S.op("act", lambda i=i: nc.scalar.activation(out=sq[i][:], in_=xt[i][:], func=AF.Square),
                         r=[bx], w=[bsq])
                    bk, bkb = pp.next()
                    for dc in range(DC):
                        S.op("pe", lambda i=i, dc=dc, bk=bk: nc.tensor.matmul(
                            bk[:], cb["ones"][:], sq[i][:, dc, :], start=(dc == 0), stop=(dc == DC - 1)),
                            r=[bsq, CB], w=[bkb])
                    S.op("act", lambda i=i, bk=bk: nc.scalar.activation(
                        out=rs[i][:], in_=bk[:], func=AF.Ln, bias=epsb[:, 0:1], scale=1.0 / D), r=[bkb, CB], w=[brs])
                    S.op("act", lambda i=i: nc.scalar.activation(
                        out=rs[i][:], in_=rs[i][:], func=AF.Exp, scale=-0.5), r=[brs], w=[brs])
                    for dc in range(DC):
                        S.op("dve", lambda i=i, dc=dc: nc.vector.scalar_tensor_tensor(
                            hh[i][:, dc, :], xt[i][:, dc, :], smalls[:, gcol + dc:gcol + dc + 1], rs[i][:],
                            ALU.mult, ALU.mult), r=[bx, brs, CB], w=[bh])
                    S.dma("sp", lambda i=i, cols=cols: nc.sync.dma_start(
                        out=hT[:, :, cols].rearrange("c p t -> p c t"), in_=hh[i][:]), r=[bh], w=[B("hT", g)])
            S.barrier()

        def load_qkv_w(st, L, kind):
            wsrc = w_qkv[L]
            ncolw = 3 * D + (640 if kind == 1 else 0)
            wt = sb(st, "q_w", [128, DC, ncolw], BF16)
            WB = B("q_w")
            wwi = None
            for dc in range(DC):
                rows = slice(dc * 128, (dc + 1) * 128)
                wload(wt[:, dc, 0:3 * D], wsrc[rows, 0:3 * D], WB)
                if kind == 1:
                    wload(wt[:, dc, 3072:3584], wsrc[rows, 3072:3584], WB)
                    wload(wt[:, dc, 3584:3648], wsrc[rows, 3584:3648], WB)
                    wload(wt[:, dc, 3648:3712], wsrc[rows, 3584:3648], WB)
            if kind == 1:
                wwi = sb(st, "q_wwi", [128, DC, 8], BF16)
                for dc in range(DC):
                    wload(wwi[:, dc, :], wsrc[dc * 128:(dc + 1) * 128, 3648:3656], WB)
            return wt, wwi

        def pass_qkv(L, kind, wt, wwi):
            WB = B("q_w")
            with contextlib.ExitStack() as st:
                hh = [sb(st, "q_h%d" % i, [128, DC, GW], BF16) for i in range(2)]
                sqt = [sb(st, "q_sq%d" % i, [128, GW], BF16) for i in range(3)]
                rst = [sb(st, "q_rs%d" % i, [128, GW], F32) for i in range(3)]
                xnt = [sb(st, "q_xn%d" % i, [128, GW], BF16) for i in range(3)]
                t1t = [sb(st, "q_t1%d" % i, [128, GW], F32) for i in range(3)]
                t2t = [sb(st, "q_t2%d" % i, [128, GW], F32) for i in range(3)]
                outt = [sb(st, "q_o%d" % i, [128, GW], BF16) for i in range(6)]
                vt = [sb(st, "q_v%d" % i, [128, D], BF16) for i in range(2)]
                wit = [sb(st, "q_wi%d" % i, [128, 8], F32) for i in range(2)]
                qk8 = sb(st, "q_qk8", [128, 1], F32)
                if kind == 1:
                    rc = [sb(st, "q_rc%d" % i, [128, GW], F32) for i in range(2)]
                    rsn = [sb(st, "q_rsn%d" % i, [128, GW], F32) for i in range(2)]
                r_sq, r_rs, r_xn, r_t1, r_t2, r_out = (
                    Rot([0, 1, 2]), Rot([0, 1, 2]), Rot([0, 1, 2]), Rot([0, 1, 2]), Rot([0, 1, 2]), Rot([0, 1, 2, 3, 4, 5]))
                pp = psum_pool(range(8))
                pp_proj, pp_n, pp_r = psum_pool([0, 1, 2]), psum_pool([3, 4]), psum_pool([5, 6])
                qpipe = []

                def _qadv():
                    for ent in reversed(qpipe):
                        ent[0][ent[1]]()
                        ent[1] += 1
                    qpipe[:] = [ent for ent in qpipe if ent[1] < len(ent[0])]

                def qpush(stages):
                    qpipe.append([list(stages), 0])
                    _qadv()
                if kind != 2:
                    S.op("dve", lambda: nc.vector.tensor_scalar(
                        qk8[:], smalls[:, SM_QK + 2 * L:SM_QK + 2 * L + 1], 0.125, None, ALU.mult),
                        r=[CB], w=[B("q_qk8")])
                def load_h(g):
                    i = g % 2
                    cols = slice(g * GW, (g + 1) * GW)
                    S.dma("sp", lambda: nc.sync.dma_start(
                        out=hh[i][:], in_=hT[:, :, cols].rearrange("c p t -> p c t")), w=[B("q_h", i)])
                    if kind == 1:
                        S.dma("sp", lambda: nc.sync.dma_start(out=rc[i][:], in_=rcos_d[:, cols]), w=[B("q_rc", i)])
                        S.dma("sp", lambda: nc.sync.dma_start(out=rsn[i][:], in_=rsin_d[:, cols]), w=[B("q_rc", i)])

                load_h(0)
                for g in range(NG):
                    i = g % 2
                    cols = slice(g * GW, (g + 1) * GW)
                    bh = B("q_h", i)
                    if kind == 1:
                        brc = B("q_rc", i)
                    if g + 1 < NG:
                        load_h(g + 1)
                    fm = []
                    for c in range(8):
                        fm.append((qT, c, c * 128, "q"))
                    for c in range(8):
                        fm.append((kT, c, D + c * 128, "k"))
                    if kind == 1:
                        for c in range(4):
                            fm.append((qiT, c, 3072 + c * 128, "qi"))
                        fm.append((kiT, None, 3584, "ki"))
                    for (dst, c, w0, role) in fm:
                        bk, bkb = pp_proj.next()
                        io = r_out.next()
                        bo = B("q_o", io)
                        normed = (kind != 2) and role in ("q", "k")
                        roped = (kind == 1)
                        isq = irs = ixn = it1 = it2 = None
                        bk2 = bkb2 = bk3 = bkb3 = None
                        if normed:
                            isq, irs = r_sq.next(), r_rs.next()
                            bk2, bkb2 = pp_n.next()
                        if roped:
                            ixn, it1, it2 = r_xn.next(), r_t1.next(), r_t2.next()
                            bk3, bkb3 = pp_r.next()
                        gain = qk8[:, 0:1] if role == "q" else smalls[:, SM_QK + 2 * L + 1:SM_QK + 2 * L + 2]
                        brc_ = B("q_rc", i)

                        def store(io=io, bo=bo, cols=cols, dst=dst, c=c, role=role, g=g):
                            if role == "ki":
                                S.dma("sp", lambda: nc.sync.dma_start(out=kiT[:, cols], in_=outt[io][:]),
                                      r=[bo], w=[B("kiT", g)])
                            else:
                                S.dma("sp", lambda: nc.sync.dma_start(out=dst[c, :, cols], in_=outt[io][:]),
                                      r=[bo], w=[B("fm", role, c, g)])

                        def st0(bk=bk, bkb=bkb, i=i, w0=w0, bh=bh, normed=normed, roped=roped, isq=isq, ixn=ixn, io=io,
                                bo=bo, role=role, store=store):
                            for dc in range(DC):
                                S.op("pe", lambda dc=dc: nc.tensor.matmul(
                                    bk[:], wt[:, dc, w0:w0 + 128], hh[i][:, dc, :], start=(dc == 0), stop=(dc == DC - 1)),
                                    r=[bh, WB], w=[bkb])
                            if normed:
                                S.op("act", lambda: nc.scalar.activation(out=sqt[isq][:], in_=bk[:], func=AF.Square),
                                     r=[bkb], w=[B("q_sq", isq)])
                            elif roped:
                                S.op("act", lambda: nc.scalar.copy(out=xnt[ixn][:], in_=bk[:]), r=[bkb], w=[B("q_xn", ixn)])
                            else:
                                sc = 0.125 if role == "q" else 1.0
                                S.op("act", lambda: nc.scalar.mul(out=outt[io][:], in_=bk[:], mul=sc), r=[bkb], w=[bo])
                                store()

                        def st1(bk=bk, bkb=bkb, bk2=bk2, bkb2=bkb2, normed=normed, roped=roped, isq=isq, irs=irs, ixn=ixn,
                                io=io, bo=bo, gain=gain, store=store):
                            if not normed:
                                return
                            bsq, brs = B("q_sq", isq), B("q_rs", irs)
                            S.op("pe", lambda: nc.tensor.matmul(
                                bk2[:], cb["blockones"][:], sqt[isq][:], start=True, stop=True), r=[bsq, CB], w=[bkb2])
                            S.op("act", lambda: nc.scalar.activation(
                                out=rst[irs][:], in_=bk2[:], func=AF.Ln, bias=epsb[:, 0:1], scale=1.0 / 64),
                                r=[bkb2, CB], w=[brs])
                            S.op("act", lambda: nc.scalar.activation(
                                out=rst[irs][:], in_=rst[irs][:], func=AF.Exp, scale=-0.5), r=[brs], w=[brs])
                            tgt = xnt[ixn] if roped else outt[io]
                            btgt = B("q_xn", ixn) if roped else bo
                            S.op("dve", lambda: nc.vector.scalar_tensor_tensor(
                                tgt[:], bk[:], gain, rst[irs][:], ALU.mult, ALU.mult),
                                r=[bkb, brs, B("q_qk8"), CB], w=[btgt])
                            if not roped:
                                store()

                        def st2(bk3=bk3, bkb3=bkb3, roped=roped, ixn=ixn, it1=it1, it2=it2, io=io, bo=bo, i=i, brc_=brc_,
                                store=store):
                            if not roped:
                                return
                            bxn, bt1, bt2 = B("q_xn", ixn), B("q_t1", it1), B("q_t2", it2)
                            S.op("pe", lambda: nc.tensor.matmul(
                                bk3[:], cb["pmat"][:], xnt[ixn][:], start=True, stop=True), r=[bxn, CB], w=[bkb3])
                            S.op("pool", lambda: nc.gpsimd.tensor_tensor(
                                t1t[it1][:], xnt[ixn][:], rc[i][:], ALU.mult), r=[bxn, brc_], w=[bt1])
                            S.op("dve", lambda: nc.vector.tensor_tensor(
                                t2t[it2][:], bk3[:], rsn[i][:], ALU.mult), r=[bkb3, brc_], w=[bt2])
                            S.op("pool", lambda: nc.gpsimd.tensor_tensor(
                                outt[io][:], t1t[it1][:], t2t[it2][:], ALU.add), r=[bt1, bt2], w=[bo])
                            store()

                        qpush([st0, st1, st2])
                    while qpipe:
                        _qadv()
                    for tt in range(4):
                        iv = (g * 4 + tt) % 2
                        bv = B("q_v", iv)
                        for half in range(2):
                            bk, bkb = pp.next()
                            for dc in range(DC):
                                S.op("pe", lambda i=i, dc=dc, bk=bk, tt=tt, half=half: nc.tensor.matmul(
                                    bk[:], hh[i][:, dc, tt * 128:(tt + 1) * 128],
                                    wt[:, dc, 2 * D + half * 512:2 * D + (half + 1) * 512],
                                    start=(dc == 0), stop=(dc == DC - 1)), r=[bh, WB], w=[bkb])
                            S.op("act", lambda iv=iv, bk=bk, half=half: nc.scalar.copy(
                                out=vt[iv][:, half * 512:(half + 1) * 512], in_=bk[:]), r=[bkb], w=[bv])
                        t0 = g * GW + tt * 128
                        if kind == 2:
                            S.dma("sp", lambda iv=iv, t0=t0: nc.sync.dma_start(out=vS[t0:t0 + 128, :], in_=vt[iv][:]),
                                  r=[bv], w=[B("vS", g, tt)])
                        else:
                            kb_ = g * 4 + tt
                            for e in range(2):
                                S.dma("sp", lambda iv=iv, kb_=kb_, e=e: nc.sync.dma_start(
                                    out=vS2[:, :, kb_, 128 * e:128 * e + 64].rearrange("h p f -> p h f"),
                                    in_=vt[iv][:].rearrange("p (h x) -> p h x", x=128)[:, :, 64 * e:64 * e + 64]),
                                    r=[bv], w=[B("vS", g, tt, e)])
                        if kind == 1:
                            bw = B("q_wi", iv)
                            bk, bkb = pp.next()
                            for dc in range(DC):
                                S.op("pe", lambda i=i, dc=dc, bk=bk, tt=tt: nc.tensor.matmul(
                                    bk[:, 0:8], hh[i][:, dc, tt * 128:(tt + 1) * 128], wwi[:, dc, :],
                                    start=(dc == 0), stop=(dc == DC - 1)), r=[bh, WB], w=[bkb])
                            S.op("act", lambda iv=iv, bk=bk: nc.scalar.copy(out=wit[iv][:], in_=bk[:, 0:8]),
                                 r=[bkb], w=[bw])
                            S.dma("sp", lambda iv=iv, t0=t0: nc.sync.dma_start(out=wiS[t0:t0 + 128, :], in_=wit[iv][:]),
                                  r=[bw], w=[B("wiS", g, tt)])
                while qpipe:
                    _qadv()
            S.barrier()

        def pass_idx():
            with contextlib.ExitStack() as st:
                ki2 = sb(st, "i_ki", [128, S_LEN], BF16)
                qiz = [sb(st, "i_qiz%d" % i, [128, 2, 4, 128], BF16) for i in range(2)]
                wi = [sb(st, "i_wi%d" % i, [128, 8], F32) for i in range(2)]
                aw = [sb(st, "i_aw%d" % i, [128, 8], F32) for i in range(2)]
                sg = [sb(st, "i_sg%d" % i, [128, 8], F32) for i in range(2)]
                acc = [sb(st, "i_acc%d" % i, [128, S_LEN], F32) for i in range(2)]
                junk = sb(st, "i_junk", [128, S_LEN], BF16)
                nm = sb(st, "i_nm", [128, S_LEN], F32)
                rl = [sb(st, "i_rl%d" % i, [128, GW], F32) for i in range(3)]
                sc = [sb(st, "i_sc%d" % i, [128, 8], F32) for i in range(2)]
                nmT = [sb(st, "i_nmT%d" % i, [128, 4, 128], BF16) for i in range(2)]
                KB_ = B("i_ki")
                S.dma("sp", lambda: nc.sync.dma_start(out=ki2[:], in_=kiT[:, :]), w=[KB_])
                for i in range(2):
                    S.op("dve", lambda i=i: nc.vector.memset(qiz[i][:], 0.0), w=[B("i_qiz", i)])
                pp = psum_pool(range(8))
                r_rl = Rot([0, 1, 2])
                r_nmT = Rot([0, 1])
                cwi = (8.0 ** -0.5) * (64.0 ** -0.5)
                junk2 = sb(st, "i_junk2", [128, S_LEN], BF16)

                def cols_of(i):
                    return tuple(sc[i][:, j:j + 1] for j in range(6))

                def phase1(qt, i):
                    nk = (qt + 1) * 128
                    qc = slice(qt * 128, (qt + 1) * 128)
                    bq, bwi, bacc, bsc = B("i_qiz", i), B("i_wi", i), B("i_acc", i), B("i_sc", i)
                    for e in range(2):
                        S.dma("sp", lambda e=e: nc.sync.dma_start(
                            out=qiz[i][64 * e:64 * e + 64, e, :, :],
                            in_=qiT[:, 64 * e:64 * e + 64, qc].rearrange("c p t -> p c t")), w=[bq])
                    S.dma("sp", lambda: nc.sync.dma_start(out=wi[i][:], in_=wiS[qc, :]), w=[bwi])
                    S.op("act", lambda: nc.scalar.activation(
                        out=aw[i][:], in_=wi[i][:], func=AF.Abs, scale=cwi), r=[bwi], w=[B("i_aw", i)])
                    S.op("dve", lambda: nc.vector.tensor_scalar(
                        sg[i][:], wi[i][:], 0.0, 2.0, ALU.is_gt, ALU.mult), r=[bwi], w=[B("i_sg", i)])
                    S.op("dve", lambda: nc.vector.tensor_scalar(
                        sg[i][:], sg[i][:], -1.0, None, ALU.add), r=[B("i_sg", i)], w=[B("i_sg", i)])
                    nkc = (nk + GW - 1) // GW
                    for hh_ in range(8):
                        for kc in range(nkc):
                            k0 = kc * GW
                            k1 = min(nk, k0 + GW)
                            n = k1 - k0
                            bk, bkb = pp.next()
                            S.op("pe", lambda hh_=hh_, bk=bk, k0=k0, k1=k1, n=n: nc.tensor.matmul(
                                bk[:, 0:n], qiz[i][:, hh_ % 2, hh_ // 2, :], ki2[:, k0:k1], start=True, stop=True),
                                r=[bq, KB_], w=[bkb])
                            ir = r_rl.next()
                            br = B("i_rl", ir)
                            S.op("act", lambda hh_=hh_, bk=bk, n=n, ir=ir: nc.scalar.activation(
                                out=rl[ir][:, 0:n], in_=bk[:, 0:n], func=AF.Relu, scale=aw[i][:, hh_:hh_ + 1]),
                                r=[bkb, B("i_aw", i)], w=[br])
                            bac = B("i_acc", i, kc)
                            eng, E_ = ("dve", nc.vector)
                            if hh_ == 0:
                                S.op(eng, lambda ir=ir, k0=k0, k1=k1, n=n, E_=E_: E_.tensor_scalar(
                                    acc[i][:, k0:k1], rl[ir][:, 0:n], sg[i][:, 0:1], None, ALU.mult),
                                    r=[br, B("i_sg", i)], w=[bac])
                            else:
                                S.op(eng, lambda ir=ir, k0=k0, k1=k1, n=n, hh_=hh_, E_=E_: E_.scalar_tensor_tensor(
                                    acc[i][:, k0:k1], rl[ir][:, 0:n], sg[i][:, hh_:hh_ + 1], acc[i][:, k0:k1],
                                    ALU.mult, ALU.add), r=[br, B("i_sg", i), bac], w=[bac])
                    bacs = [B("i_acc", i, kc) for kc in range(nkc)]
                    lo, hi, mid, cnt, ge, dd = cols_of(i)
                    if nk > TOPK:
                        S.op("dve", lambda: nc.vector.tensor_reduce(hi, acc[i][:, 0:nk], AX.X, ALU.max),
                             r=bacs, w=[bsc])
                        S.op("dve", lambda: nc.vector.tensor_reduce(lo, acc[i][:, 0:nk], AX.X, ALU.min),
                             r=bacs, w=[bsc])
                        S.op("dve", lambda: nc.vector.tensor_scalar(lo, lo, -1.0, None, ALU.add), r=[bsc], w=[bsc])
                        S.op("dve", lambda: nc.vector.tensor_tensor(dd, hi, lo, ALU.subtract), r=[bsc], w=[bsc])
                    else:
                        S.op("dve", lambda: nc.vector.memset(lo, -1e29), w=[bsc])
                    S.op("dve", lambda: nc.vector.memset(acc[i][0:64, nk - 64:nk], -1e30), r=[bsc] + bacs, w=[bacc])

                def bis_t(qt, i, it):
                    lo, hi, mid, cnt, ge, dd = cols_of(i)
                    bsc = B("i_sc", i)
                    f = 2.0 ** -(it + 1)
                    S.op("dve", lambda: nc.vector.scalar_tensor_tensor(mid, dd, f, lo, ALU.mult, ALU.add),
                         r=[bsc], w=[bsc])

                def bis_count(qt, i, it, on_act):
                    nk = (qt + 1) * 128
                    lo, hi, mid, cnt, ge, dd = cols_of(i)
                    bsc, bacc = B("i_sc", i), B("i_acc", i)
                    bcnt = B("i_cnt", i)
                    if on_act:
                        S.op("act", lambda: nc.scalar.activation(
                            out=junk2[:, 0:nk], in_=acc[i][:, 0:nk], func=AF.Sign, scale=-1.0, bias=mid, accum_out=cnt),
                            r=[bacc, bsc], w=[bcnt, B("i_junk2")])
                    else:
                        S.op("dve", lambda: nc.vector.tensor_scalar(
                            junk[:, 0:nk], acc[i][:, 0:nk], mid, 0.0, ALU.is_gt, ALU.add, accum_out=cnt),
                            r=[bacc, bsc], w=[bcnt, B("i_junk")])

                def bis_upd(qt, i, it, on_act):
                    nk = (qt + 1) * 128
                    lo, hi, mid, cnt, ge, dd = cols_of(i)
                    bsc, bcnt = B("i_sc", i), B("i_cnt", i)
                    f = 2.0 ** -(it + 1)
                    if on_act:
                        S.op("dve", lambda: nc.vector.tensor_scalar(
                            ge, cnt, float(nk - 2 * TOPK) + 1.0, f, ALU.is_lt, ALU.mult), r=[bcnt], w=[bsc])
                    else:
                        S.op("dve", lambda: nc.vector.tensor_scalar(
                            ge, cnt, float(TOPK) - 0.5, f, ALU.is_gt, ALU.mult), r=[bcnt], w=[bsc])
                    S.op("dve", lambda: nc.vector.scalar_tensor_tensor(lo, dd, ge, lo, ALU.mult, ALU.add),
                         r=[bsc], w=[bsc])

                def phase3(qt, i):
                    nk = (qt + 1) * 128
                    qc = slice(qt * 128, (qt + 1) * 128)
                    lo = cols_of(i)[0]
                    bsc, bacc = B("i_sc", i), B("i_acc", i)
                    bnm = B("i_nm")
                    S.op("dve", lambda: nc.vector.tensor_scalar(
                        nm[:, 0:nk], acc[i][:, 0:nk], lo, None, ALU.is_gt), r=[bacc, bsc], w=[bnm])
                    for k4 in range((qt + 4) // 4):
                        nb = min(4, qt + 1 - k4 * 4)
                        bk, bkb = pp.next()
                        for j in range(nb):
                            kb = k4 * 4 + j
                            S.op("pe", lambda bk=bk, j=j, kb=kb: nc.tensor.transpose(
                                bk[:, j * 128:(j + 1) * 128], nm[:, kb * 128:(kb + 1) * 128], ident_f[:]),
                                r=[bnm, CB], w=[bkb])
                        it_ = r_nmT.next()
                        bt = B("i_nmT", it_)
                        S.op("act", lambda it_=it_, bk=bk, nb=nb: nc.scalar.copy(
                            out=nmT[it_][:, 0:nb, :], in_=bk[:, 0:nb * 128].rearrange("p (k t) -> p k t", t=128)),
                            r=[bkb], w=[bt])
                        S.dma("sp", lambda it_=it_, k4=k4, nb=nb: nc.sync.dma_start(
                            out=nmS[k4 * 4:k4 * 4 + nb, :, qc].rearrange("k p t -> p k t"), in_=nmT[it_][:, 0:nb, :]),
                            r=[bt], w=[B("nmS", qt, k4)])

                for p2 in range(16):
                    qa, qb = 2 * p2, 2 * p2 + 1
                    phase1(qa, 0)
                    phase1(qb, 1)
                    if (qa + 1) * 128 > TOPK:
                        for it in range(N_BISECT):
                            bis_t(qb, 1, it)
                            bis_count(qb, 1, it, True)
                            bis_t(qa, 0, it)
                            bis_count(qa, 0, it, False)
                            bis_upd(qa, 0, it, False)
                            bis_upd(qb, 1, it, True)
                    phase3(qa, 0)
                    phase3(qb, 1)
            S.barrier()

        def pass_att(L, kind, pre_issue=None):
            with contextlib.ExitStack() as st:
                kt = [sb(st, "a_k%d" % i, [128, S_LEN], BF16) for i in range(2)]
                VW = 128 if kind == 2 else 192
                vv = [sb(st, "a_v%d" % i, [128, 32, VW], BF16) for i in range(2)]
                if kind != 2:
                    Tt = [sb(st, "a_T%d" % i, [128, GW], F32) for i in range(2)]
                    ncp = [sb(st, "a_ncp%d" % i, [128, GW], F32) for i in range(2)]
                qz = [sb(st, "a_qz%d" % i, [128, 2, GW], BF16) for i in range(2)]
                pt = [sb(st, "a_p%d" % i, [128, GW], BF16) for i in range(6)] if kind != 2 else None
                numsb = [sb(st, "a_num%d" % i, [128, GW], F32) for i in range(2)] if kind != 2 else None
                ot = [sb(st, "a_o%d" % i, [128, GW], BF16) for i in range(2)]
                if kind == 0:
                    strip = sb(st, "a_strip", [128, 16, 640], BF16)
                    SB_ = B("a_strip")
                    for h4 in range(4):
                        wload(strip[:, h4 * 4:(h4 + 1) * 4, :], strips[L][:, h4 * 4:(h4 + 1) * 4, :], SB_)
                if kind == 1:
                    mk = [sb(st, "a_mk%d" % i, [128, 32, GW], BF16) for i in range(2)]
                if kind == 2:
                    et2 = sb(st, "a_e2", [128, 2 * GW], F32)
                    spt2 = [sb(st, "a_sp%d" % i, [128, 2 * GW], BF16) for i in range(3)]
                    eft2 = [sb(st, "a_ef%d" % i, [128, 2 * GW], F32) for i in range(2)]
                    pt2 = [sb(st, "a_p2%d" % i, [128, 2 * GW], BF16) for i in range(3)]
                    Rt = [sb(st, "a_R%d" % i, [128, GW], F32) for i in range(2)]
                for i in range(2):
                    S.op("dve", lambda i=i: nc.vector.memset(qz[i][:], 0.0), w=[B("a_qz", i)])
                if pre_issue is not None:
                    pre_issue()
                if kind == 2:
                    pE, pC, pA = psum_pool([0, 1, 2, 3]), psum_pool([4, 5]), psum_pool([6, 7])
                    pEp = Rot([(ppairs[0], bankB[0], bankB[1]), (ppairs[1], bankB[2], bankB[3])])
                    r_sp2, r_ef2, r_p2 = Rot([0, 1, 2]), Rot([0, 1]), Rot([0, 1, 2])
                else:
                    pS, pA = psum_pool([0, 1, 2, 3]), psum_pool([4, 5, 6, 7])
                r_p = Rot([0, 1, 2, 3, 4, 5])
                r_e, r_sp, r_ef = Rot([0, 1]), Rot([0, 1, 2, 3]), Rot([0, 1])
                pipe = []

                def _advance():
                    for ent in reversed(pipe):
                        ent[0][ent[1]]()
                        ent[1] += 1
                    pipe[:] = [ent for ent in pipe if ent[1] < len(ent[0])]

                def push(stages):
                    pipe.append([list(stages), 0])
                    _advance()

                def flush():
                    while pipe:
                        _advance()

                def load_kv(hp):
                    ik = hp % 2
                    bkt, bvv = B("a_k", ik), B("a_v", ik)
                    S.dma("sp", lambda: nc.sync.dma_start(out=kt[ik][:], in_=kT[hp, :, :]), w=[bkt])
                    for v4 in range(4):
                        if kind == 2:
                            S.dma("sp", lambda v4=v4: nc.sync.dma_start(
                                out=vv[ik][:, v4 * 8:(v4 + 1) * 8, :],
                                in_=vS[v4 * 1024:(v4 + 1) * 1024, hp * 128:(hp + 1) * 128].rearrange(
                                    "(k p) f -> p k f", p=128)), w=[bvv])
                        else:
                            S.dma("sp", lambda v4=v4: nc.sync.dma_start(
                                out=vv[ik][:, v4 * 8:(v4 + 1) * 8, :], in_=vS2[hp, :, v4 * 8:(v4 + 1) * 8, :]), w=[bvv])

                uid = 0
                load_kv(0)
                for hp in range(8):
                    ik = hp % 2
                    bkt, bvv = B("a_k", ik), B("a_v", ik)
                    for g in range(NG):
                        if g == 1 and hp + 1 < 8:
                            load_kv(hp + 1)
                        uid += 1
                        iq = uid % 2
                        cols = slice(g * GW, (g + 1) * GW)
                        bqz = B("a_qz", iq)
                        for e in range(2):
                            S.dma("sp", lambda iq=iq, e=e, hp=hp, cols=cols: nc.sync.dma_start(
                                out=qz[iq][64 * e:64 * e + 64, e, :], in_=qT[hp, 64 * e:64 * e + 64, cols]), w=[bqz])
                        bmk = None
                        if kind == 1:
                            bmk = B("a_mk", iq)
                            nkb = 4 * (g + 1)
                            S.dma("sp", lambda iq=iq, nkb=nkb, cols=cols: nc.sync.dma_start(
                                out=mk[iq][:, 0:nkb, :], in_=nmS[0:nkb, :, cols].rearrange("k p t -> p k t")), w=[bmk])
                        acc, accb = pA.next()
                        acc1 = acc1b = None
                        if kind != 2:
                            acc1, acc1b = pA.next()
                        io = uid % 2
                        bo = B("a_o", io)

                        def finalize(io=io, bo=bo, acc=acc, accb=accb, hp=hp, cols=cols, g=g):
                            S.op("act", lambda: nc.scalar.copy(out=ot[io][:], in_=acc[:]), r=[accb], w=[bo])
                            S.dma("sp", lambda: nc.sync.dma_start(out=oT[hp, :, cols], in_=ot[io][:]),
                                  r=[bo], w=[B("oT", hp, g)])

                        def fin_f1(io=io, acc0=acc, acc0b=accb, acc1=acc1, acc1b=acc1b):
                            bn, bn1, bt, bc = B("a_num", io), B("a_num1", io), B("a_T", io), B("a_ncp", io)
                            S.op("dve", lambda: nc.vector.reciprocal(numsb[io][64:128, :], acc0[64:128, :]), r=[acc0b], w=[bn])
                            S.op("act", lambda: nc.scalar.activation(
                                out=numsb[io][0:64, :], in_=acc1[0:64, :], func=AF.Ln), r=[acc1b], w=[bn1])
                            S.op("act", lambda: nc.scalar.activation(
                                out=numsb[io][0:64, :], in_=numsb[io][0:64, :], func=AF.Exp, scale=-1.0), r=[bn1], w=[bn1])
                            S.op("act", lambda: nc.scalar.copy(out=ncp[io][0:64, :], in_=acc0[0:64, :]), r=[acc0b], w=[bc])
                            S.op("act", lambda: nc.scalar.copy(out=ncp[io][64:128, :], in_=acc1[64:128, :]), r=[acc1b], w=[bc])
                            S.dma("sp", lambda: nc.sync.dma_start(out=Tt[io][0:64, :], in_=numsb[io][64:128, :]), r=[bn], w=[bt])
                            S.dma("sp", lambda: nc.sync.dma_start(out=Tt[io][64:128, :], in_=numsb[io][0:64, :]), r=[bn1], w=[bt])

                        def fin_f2(io=io, bo=bo, hp=hp, cols=cols, g=g):
                            bt, bc = B("a_T", io), B("a_ncp", io)
                            S.op("dve", lambda: nc.vector.tensor_tensor(
                                ot[io][:], ncp[io][:], Tt[io][:], ALU.mult), r=[bt, bc], w=[bo])
                            S.dma("sp", lambda: nc.sync.dma_start(out=oT[hp, :, cols], in_=ot[io][:]),
                                  r=[bo], w=[B("oT", hp, g)])

                        for e in range(2):
                            h = 2 * hp + e
                            ps_ = slice(64 * e, 64 * e + 64)
                            if kind == 0:
                                tiles = []
                                for kbr in [4, 3, 2, 1, 0, 5, 6, 7]:
                                    kb = 4 * g - 4 + kbr
                                    if kb < 0:
                                        continue
                                    tiles.append((kb, max(0, 128 * (kbr - 4)), min(512, 128 * (kbr + 1)), 512 - 128 * kbr))
                            else:
                                kbs = list(range(4 * g + 3, -1, -1))
                                if kind == 1:
                                    kbs = [4 * g, 4 * g + 1, 4 * g + 2, 4 * g + 3] + list(range(4 * g - 1, -1, -1))
                                tiles = []
                                for kb in kbs:
                                    j = kb - 4 * g
                                    tiles.append((kb, 128 * j if j > 0 else 0, 512, None))
                            iR = e
                            bR = B("a_R", iR)
                            if kind == 2:
                                units = []
                                ti = 0
                                while ti < len(tiles):
                                    kb = tiles[ti][0]
                                    if kb >= 4 * g or ti + 1 >= len(tiles):
                                        units.append([ti])
                                        ti += 1
                                    else:
                                        units.append([ti, ti + 1])
                                        ti += 2
                                for un in units:
                                    tl = [dict(kb=tiles[x][0], t0=tiles[x][1], t1=tiles[x][2], n=tiles[x][2] - tiles[x][1],
                                               first=(x == 0), last=(x == len(tiles) - 1), diag=(tiles[x][0] >= 4 * g))
                                          for x in un]
                                    m = len(tl)
                                    isp, ief, ip = r_sp2.next(), r_ef2.next(), r_p2.next()
                                    bsp, bef, bp, be = B("a_sp", isp), B("a_ef", ief), B("a_p2", ip), B("a_e2")
                                    if m == 2:
                                        pfull, pb0, pb1 = pEp.next()
                                        ebks = [(pfull[:, 0:GW], pb0), (pfull[:, GW:2 * GW], pb1)]
                                    else:
                                        pfull = None
                                        ebks = [pE.next()]
                                    cbk, cbb = pC.next()
                                    W = 2 * GW if m == 2 else tl[0]["n"]
                                    lastunit = tl[-1]["last"]
                                    fin = finalize if (lastunit and e == 1) else None

                                    def stA(tl=tl, m=m, isp=isp, bsp=bsp, be=be, ebks=ebks, pfull=pfull, W=W, ik=ik, iq=iq, e=e,
                                            iR=iR, bR=bR, bkt=bkt, bqz=bqz):
                                        if tl[0]["first"]:
                                            S.op("dve", lambda: nc.vector.memset(Rt[iR][:], 0.0), w=[bR])
                                        for j, t in enumerate(tl):
                                            ebk, ebb = ebks[j]
                                            S.op("pe", lambda ebk=ebk, t=t: nc.tensor.matmul(
                                                ebk[:, t["t0"]:t["t1"]], kt[ik][:, t["kb"] * 128:(t["kb"] + 1) * 128],
                                                qz[iq][:, e, t["t0"]:t["t1"]], start=True, stop=False, skip_group_check=True),
                                                r=[bkt, bqz], w=[ebb])
                                        if m == 2:
                                            S.op("act", lambda: nc.scalar.activation(
                                                out=et2[:, 0:W], in_=pfull[:, 0:W], func=AF.Exp),
                                                r=[ebks[0][1], ebks[1][1]], w=[be])
                                        else:
                                            t = tl[0]
                                            S.op("act", lambda: nc.scalar.activation(
                                                out=et2[:, 0:W], in_=ebks[0][0][:, t["t0"]:t["t1"]], func=AF.Exp),
                                                r=[ebks[0][1]], w=[be])
                                        S.op("act", lambda: nc.scalar.activation(
                                            out=spt2[isp][:, 0:W], in_=et2[:, 0:W], func=AF.Ln, bias=1.0), r=[be], w=[bsp])
                                        if tl[0]["diag"]:
                                            S.op("dve", lambda: nc.vector.tensor_tensor(
                                                spt2[isp][:, 0:128], spt2[isp][:, 0:128], cb["tri01"][:], ALU.mult),
                                                r=[bsp, CB], w=[bsp])

                                    def stB(tl=tl, m=m, isp=isp, bsp=bsp, ief=ief, bef=bef, ip=ip, bp=bp, ebks=ebks, cbk=cbk, cbb=cbb,
                                            W=W, iR=iR, bR=bR, lastunit=lastunit):
                                        for j, t in enumerate(tl):
                                            ebk, ebb = ebks[j]
                                            n, t0, t1 = t["n"], t["t0"], t["t1"]
                                            more = t["diag"] or (m == 2 and j == 1)
                                            S.op("pe", lambda ebk=ebk, j=j, n=n, t0=t0, t1=t1, more=more: nc.tensor.matmul(
                                                ebk[:, t0:t1], cb["negu"][:], spt2[isp][:, j * GW:j * GW + n], start=False,
                                                stop=(not more), skip_group_check=True), r=[bsp, CB], w=[ebb])
                                            if m == 2 and j == 1:
                                                S.op("pe", lambda ebk=ebk: nc.tensor.matmul(
                                                    ebk[:, 0:GW], cb["negones"][:], spt2[isp][:, 0:GW], start=False, stop=True,
                                                    skip_group_check=True), r=[bsp, CB], w=[ebb])
                                            if t["diag"]:
                                                S.op("pe", lambda ebk=ebk, t0=t0: nc.tensor.matmul(
                                                    ebk[:, t0:t0 + 128], ident_b[:], cb["negtri"][:], start=False, stop=True,
                                                    skip_group_check=True), r=[CB], w=[ebb])
                                        t0, t1 = tl[0]["t0"], tl[0]["t1"]
                                        if not lastunit:
                                            for j, t in enumerate(tl):
                                                S.op("pe", lambda j=j, t=t: nc.tensor.matmul(
                                                    cbk[:, t0:t1], cb["negones"][:], spt2[isp][:, j * GW:j * GW + t["n"]],
                                                    start=(j == 0), stop=(j == m - 1)), r=[bsp, CB], w=[cbb])
                                        for j, t in enumerate(tl):
                                            ebk, ebb = ebks[j]
                                            S.op("dve", lambda ebk=ebk, j=j, t=t: nc.vector.tensor_tensor(
                                                eft2[ief][:, j * GW:j * GW + t["n"]], ebk[:, t["t0"]:t["t1"]],
                                                Rt[iR][:, t["t0"]:t["t1"]], ALU.add), r=[ebb, bR], w=[bef])
                                        S.op("act", lambda: nc.scalar.activation(
                                            out=pt2[ip][:, 0:W], in_=eft2[ief][:, 0:W], func=AF.Exp), r=[bef], w=[bp])
                                        if not lastunit:
                                            S.op("dve", lambda: nc.vector.tensor_tensor(
                                                Rt[iR][:, t0:t1], cbk[:, t0:t1], Rt[iR][:, t0:t1], ALU.add), r=[cbb, bR], w=[bR])

                                    def stC(tl=tl, ip=ip, bp=bp, acc=acc, accb=accb, ps_=ps_, ik=ik, e=e, bvv=bvv, fin=fin):
                                        for j, t in enumerate(tl):
                                            S.op("pe", lambda j=j, t=t: nc.tensor.matmul(
                                                acc[ps_, t["t0"]:t["t1"]], vv[ik][:, t["kb"], 64 * e:64 * e + 64],
                                                pt2[ip][:, j * GW:j * GW + t["n"]], start=t["first"], stop=t["last"],
                                                skip_group_check=True), r=[bvv, bp], w=[accb])
                                        if fin is not None:
                                            fin()

                                    push([stA, stB, stC])
                                continue
                            for ti, (kb, t0, t1, u0) in enumerate(tiles):
                                n = t1 - t0
                                first = (ti == 0)
                                last = (ti == len(tiles) - 1)
                                fin = finalize if (last and e == 1) else None
                                kcol = slice(kb * 128, (kb + 1) * 128)
                                ip = r_p.next()
                                bp = B("a_p", ip)
                                if kind in (0, 1):
                                    sbk, sbb = pS.next()

                                    def stA(sbk=sbk, sbb=sbb, ik=ik, kcol=kcol, iq=iq, e=e, t0=t0, t1=t1, n=n, h=h, u0=u0,
                                            kb=kb, ip=ip, bp=bp, bkt=bkt, bqz=bqz, bmk=bmk):
                                        S.op("pe", lambda: nc.tensor.matmul(
                                            sbk[:, t0:t1], kt[ik][:, kcol], qz[iq][:, e, t0:t1], start=True, stop=(kind == 1)),
                                            r=[bkt, bqz], w=[sbb])
                                        if kind == 0:
                                            S.op("pe", lambda: nc.tensor.matmul(
                                                sbk[:, t0:t1], ident_b[:], strip[:, h, u0 + t0:u0 + t1], start=False, stop=True),
                                                r=[SB_, CB], w=[sbb])
                                        S.op("act", lambda: nc.scalar.activation(
                                            out=pt[ip][:, 0:n], in_=sbk[:, t0:t1], func=AF.Exp), r=[sbb], w=[bp])

                                    def stB(ip=ip, bp=bp, n=n, iq=iq, kb=kb, t0=t0, t1=t1, bmk=bmk):
                                        S.op("dve", lambda: nc.vector.tensor_tensor(
                                            pt[ip][:, 0:n], pt[ip][:, 0:n], mk[iq][:, kb, t0:t1], ALU.mult),
                                            r=[bp, bmk], w=[bp])

                                    acc_e, acc_eb = (acc, accb) if e == 0 else (acc1, acc1b)
                                    f1 = fin_f1 if (last and e == 1) else None
                                    f2 = fin_f2 if (last and e == 1) else None

                                    def stC(acc_e=acc_e, acc_eb=acc_eb, ik=ik, kb=kb, e=e, ip=ip, bp=bp,
                                            t0=t0, t1=t1, n=n, first=first, last=last, bvv=bvv, f1=f1):
                                        S.op("pe", lambda: nc.tensor.matmul(
                                            acc_e[:, t0:t1], vv[ik][:, kb, 64 * e:64 * e + 128], pt[ip][:, 0:n],
                                            start=first, stop=last), r=[bvv, bp], w=[acc_eb])
                                        if f1 is not None:
                                            f1()

                                    stages = [stA, stB, stC] if kind == 1 else [stA, stC]
                                    if f2 is not None:
                                        stages = stages + [(lambda: None)] * 4 + [f2]
                                    push(stages)
                flush()
            S.barrier()

        def pass_wo(L):
            gcol = SM_N2 + 8 * L
            with contextlib.ExitStack() as st:
                wt = sb(st, "o_w", [128, DC, D], BF16)
                WB = B("o_w")
                for dc in range(DC):
                    wload(wt[:, dc, :], w_o[L][dc * 128:(dc + 1) * 128, :], WB)
                oo = [sb(st, "o_o%d" % i, [128, DC, GW], BF16) for i in range(2)]
                xt = [sb(st, "o_x%d" % i, [128, DC, GW], F32) for i in range(1)]
                sq = sb(st, "o_sq", [128, DC, GW], BF16)
                hh = sb(st, "o_h", [128, DC, GW], BF16)
                rs = sb(st, "o_rs", [128, GW], F32)
                pp = psum_pool(range(8))
                bsq, bh, brs = B("o_sq"), B("o_h"), B("o_rs")
                def load_o(g):
                    i = g % 2
                    cols = slice(g * GW, (g + 1) * GW)
                    S.dma("sp", lambda: nc.sync.dma_start(
                        out=oo[i][:], in_=oT[:, :, cols].rearrange("c p t -> p c t")), w=[B("o_o", i)])

                load_o(0)
                for g in range(NG):
                    i = g % 2
                    cols = slice(g * GW, (g + 1) * GW)
                    bo, bx = B("o_o", i), B("o_x", 0)
                    if g + 1 < NG:
                        load_o(g + 1)
                    S.dma("sp", lambda cols=cols: nc.sync.dma_start(
                        out=xt[0][:], in_=xT[:, :, cols].rearrange("c p t -> p c t")), r=[B("xT", g)], w=[bx])
                    for dch in range(DC):
                        bk, bkb = pp.next()
                        for c in range(DC):
                            S.op("pe", lambda i=i, c=c, dch=dch, bk=bk: nc.tensor.matmul(
                                bk[:], wt[:, c, dch * 128:(dch + 1) * 128], oo[i][:, c, :],
                                start=(c == 0), stop=(c == DC - 1)), r=[bo, WB], w=[bkb])
                        S.op("dve", lambda dch=dch, bk=bk: nc.vector.tensor_tensor(
                            xt[0][:, dch, :], bk[:], xt[0][:, dch, :], ALU.add), r=[bkb, bx], w=[bx])
                    S.dma("sp", lambda cols=cols: nc.sync.dma_start(
                        out=xT[:, :, cols].rearrange("c p t -> p c t"), in_=xt[0][:]), r=[bx], w=[B("xT", g)])
                    S.op("act", lambda: nc.scalar.activation(out=sq[:], in_=xt[0][:], func=AF.Square), r=[bx], w=[bsq])
                    bk, bkb = pp.next()
                    for dc in range(DC):
                        S.op("pe", lambda dc=dc, bk=bk: nc.tensor.matmul(
                            bk[:], cb["ones"][:], sq[:, dc, :], start=(dc == 0), stop=(dc == DC - 1)),
                            r=[bsq, CB], w=[bkb])
                    S.op("act", lambda bk=bk: nc.scalar.activation(
                        out=rs[:], in_=bk[:], func=AF.Ln, bias=epsb[:, 0:1], scale=1.0 / D), r=[bkb, CB], w=[brs])
                    S.op("act", lambda: nc.scalar.activation(out=rs[:], in_=rs[:], func=AF.Exp, scale=-0.5),
                         r=[brs], w=[brs])
                    for dc in range(DC):
                        S.op("dve", lambda dc=dc: nc.vector.scalar_tensor_tensor(
                            hh[:, dc, :], xt[0][:, dc, :], smalls[:, gcol + dc:gcol + dc + 1], rs[:],
                            ALU.mult, ALU.mult), r=[bx, brs, CB], w=[bh])
                    S.dma("sp", lambda cols=cols: nc.sync.dma_start(
                        out=hT[:, :, cols].rearrange("c p t -> p c t"), in_=hh[:]), r=[bh], w=[B("hT", g)])
            S.barrier()

        def alloc_ffn_w(st):
            w1 = sb(st, "f_w1", [128, DC, 2 * DFF], BF16)
            w2 = sb(st, "f_w2", [128, NJ, D], BF16)
            return w1, w2

        def issue_ffn_w2(L, w2):
            W2B = B("f_w2")
            for j2 in range(0, NJ, 2):
                wload(w2[:, j2:j2 + 2, :], w_dn[L][j2 * 128:(j2 + 2) * 128, :].rearrange("(j p) d -> p j d", p=128), W2B)

        def issue_ffn_w(L, w1, w2, skip_w2=False):
            W1B = B("f_w1")
            for q4 in range(4):
                c0, c1 = q4 * 704, (q4 + 1) * 704
                for hf in range(2):
                    for dc in range(DC):
                        wload(w1[:, dc, hf * DFF + c0:hf * DFF + c1],
                              w_in[L][dc * 128:(dc + 1) * 128, hf * DFF + c0:hf * DFF + c1], W1B)
            if not skip_w2:
                issue_ffn_w2(L, w2)

        def pass_ffn(L, final, w1, w2):
            W1B, W2B = B("f_w1"), B("f_w2")
            with contextlib.ExitStack() as st:
                hh = [sb(st, "f_h%d" % i, [128, DC, GW], BF16) for i in range(2)]
                xc = [sb(st, "f_xc%d" % i, [128, GW], F32) for i in range(2)]
                mm = sb(st, "f_m", [128, NJ, GW], BF16)
                ab = [sb(st, "f_ab%d" % i, [128, GW + 2], F32) for i in range(3)]
                t1 = [sb(st, "f_t1%d" % i, [128, GW], F32) for i in range(2)]
                t2 = [sb(st, "f_t2%d" % i, [128, GW], F32) for i in range(2)]
                cg = [sb(st, "f_cg%d" % i, [128, GW], F32) for i in range(2)]
                cu = [sb(st, "f_cu%d" % i, [128, GW], F32) for i in range(2)]
                hs = sb(st, "f_hs", [128, NFC, 2], F32)
                yt = sb(st, "f_y", [128, 4, 128], F32) if final else None
                S.op("dve", lambda: nc.vector.memset(hs[:], 0.0), w=[B("f_hs", fc) for fc in range(NFC)])
                pp = psum_pool(range(8))
                r_ab, r_t1, r_t2, r_xc = Rot([0, 1, 2]), Rot([0, 1]), Rot([0, 1]), Rot([0, 1])
                cw = lambda tap, fc: smalls[:, SM_CW + (L * 3 + tap) * 44 + fc:SM_CW + (L * 3 + tap) * 44 + fc + 1]
                cbias = lambda fc: smalls[:, SM_CB + L * 44 + fc:SM_CB + L * 44 + fc + 1]
                def load_h(g):
                    i = g % 2
                    cols = slice(g * GW, (g + 1) * GW)
                    S.dma("sp", lambda: nc.sync.dma_start(
                        out=hh[i][:], in_=hT[:, :, cols].rearrange("c p t -> p c t")), w=[B("f_h", i)])

                load_h(0)
                for g in range(NG):
                    i = g % 2
                    cols = slice(g * GW, (g + 1) * GW)
                    bh = B("f_h", i)
                    if g + 1 < NG:
                        load_h(g + 1)
                    for j in range(NJ):
                        ipair = (g * NJ + j) % 2
                        for part, fc in ((0, j), (1, j + NJ)):
                            bk, bkb = pp.next()
                            for dc in range(DC):
                                S.op("pe", lambda i=i, dc=dc, bk=bk, fc=fc: nc.tensor.matmul(
                                    bk[:], w1[:, dc, fc * 128:(fc + 1) * 128], hh[i][:, dc, :],
                                    start=(dc == 0), stop=(dc == DC - 1)), r=[bh, W1B], w=[bkb])
                            ia, i1, i2 = r_ab.next(), r_t1.next(), r_t2.next()
                            ba, b1, b2, bhs = B("f_ab", ia), B("f_t1", i1), B("f_t2", i2), B("f_hs", fc)
                            S.op("pool", lambda ia=ia, fc=fc: nc.gpsimd.tensor_copy(out=ab[ia][:, 0:2], in_=hs[:, fc, :]),
                                 r=[bhs], w=[ba])
                            S.op("act", lambda ia=ia, bk=bk: nc.scalar.copy(out=ab[ia][:, 2:GW + 2], in_=bk[:]),
                                 r=[bkb], w=[ba])
                            S.op("pool", lambda ia=ia, fc=fc: nc.gpsimd.tensor_copy(out=hs[:, fc, :], in_=ab[ia][:, GW:GW + 2]),
                                 r=[ba], w=[bhs])
                            S.op("act", lambda ia=ia, i1=i1, fc=fc: nc.scalar.activation(
                                out=t1[i1][:], in_=ab[ia][:, 0:GW], func=AF.Identity, bias=cbias(fc), scale=cw(0, fc)),
                                r=[ba, CB], w=[b1])
                            S.op("dve", lambda ia=ia, i1=i1, i2=i2, fc=fc: nc.vector.scalar_tensor_tensor(
                                t2[i2][:], ab[ia][:, 1:GW + 1], cw(1, fc), t1[i1][:], ALU.mult, ALU.add),
                                r=[ba, b1, CB], w=[b2])
                            dst, bdst = (cg[ipair], B("f_cg", ipair)) if part == 0 else (cu[ipair], B("f_cu", ipair))
                            S.op("dve", lambda ia=ia, i2=i2, fc=fc, dst=dst: nc.vector.scalar_tensor_tensor(
                                dst[:], ab[ia][:, 2:GW + 2], cw(2, fc), t2[i2][:], ALU.mult, ALU.add),
                                r=[ba, b2, CB], w=[bdst])
                        bcg = B("f_cg", ipair)
                        S.op("act", lambda ipair=ipair: nc.scalar.activation(
                            out=cg[ipair][:], in_=cg[ipair][:], func=AF.Silu), r=[bcg], w=[bcg])
                        S.op("dve", lambda ipair=ipair, j=j: nc.vector.tensor_tensor(
                            mm[:, j, :], cg[ipair][:], cu[ipair][:], ALU.mult), r=[bcg, B("f_cu", ipair)], w=[B("f_m", j)])
                    for dch in range(DC):
                        ix = r_xc.next()
                        bx = B("f_xc", ix)
                        S.dma("sp", lambda ix=ix, dch=dch, cols=cols: nc.sync.dma_start(
                            out=xc[ix][:], in_=xT[dch, :, cols]), r=[B("xT", dch, g)], w=[bx])
                        bk, bkb = pp.next()
                        for j in range(NJ):
                            S.op("pe", lambda j=j, dch=dch, bk=bk: nc.tensor.matmul(
                                bk[:], w2[:, j, dch * 128:(dch + 1) * 128], mm[:, j, :],
                                start=(j == 0), stop=(j == NJ - 1)), r=[B("f_m", j), W2B], w=[bkb])
                        S.op("dve", lambda ix=ix, bk=bk: nc.vector.tensor_tensor(
                            xc[ix][:], bk[:], xc[ix][:], ALU.add), r=[bkb, bx], w=[bx])
                        if not final:
                            S.dma("sp", lambda ix=ix, dch=dch, cols=cols: nc.sync.dma_start(
                                out=xT[dch, :, cols], in_=xc[ix][:]), r=[bx], w=[B("xT", dch, g)])
                        else:
                            bk2, bkb2 = pp.next()
                            for tt in range(4):
                                S.op("pe", lambda bk2=bk2, ix=ix, tt=tt: nc.tensor.transpose(
                                    bk2[:, tt * 128:(tt + 1) * 128], xc[ix][:, tt * 128:(tt + 1) * 128], ident_f[:]),
                                    r=[bx, CB], w=[bkb2])
                            by = B("f_y")
                            S.op("act", lambda bk2=bk2: nc.scalar.copy(
                                out=yt[:], in_=bk2[:].rearrange("p (t d) -> p t d", d=128)), r=[bkb2], w=[by])
                            S.dma("sp", lambda g=g, dch=dch: nc.sync.dma_start(
                                out=y_out[g * GW:(g + 1) * GW, dch * 128:(dch + 1) * 128].rearrange("(t p) d -> p t d", p=128),
                                in_=yt[:]), r=[by], w=[B("y", g, dch)])
            S.barrier()

        for L in range(n_layers):
            kind = L % 3
            with contextlib.ExitStack() as wst:
                wt, wwi = load_qkv_w(wst, L, kind)
                pass_norm(SM_N1 + 8 * L, first=(L == 0))
                pass_qkv(L, kind, wt, wwi)
            if kind == 1:
                pass_idx()
            with contextlib.ExitStack() as wst:
                if kind == 2:
                    w1, w2 = alloc_ffn_w(wst)
                    pass_att(L, kind, pre_issue=lambda: issue_ffn_w(L, w1, w2))
                elif kind == 0:
                    w1 = sb(wst, "f_w1", [128, DC, 2 * DFF], BF16)
                    pass_att(L, kind, pre_issue=lambda: issue_ffn_w(L, w1, None, skip_w2=True))
                    w2 = sb(wst, "f_w2", [128, NJ, D], BF16)
                    issue_ffn_w2(L, w2)
                else:
                    w2 = sb(wst, "f_w2", [128, NJ, D], BF16)
                    pass_att(L, kind, pre_issue=lambda: issue_ffn_w2(L, w2))
                    w1 = sb(wst, "f_w1", [128, DC, 2 * DFF], BF16)
                    issue_ffn_w(L, w1, w2, skip_w2=True)
                pass_wo(L)
                pass_ffn(L, (L == n_layers - 1), w1, w2)
        S.emit()
    return nc


_CACHE = {}


def make_in_maps(inputs, n_layers=4):
    c = host_consts()
    idx, band = strip_index()
    sm = pack_smalls(inputs)
    shared = {"smalls": sm, "rcos": c["rcos"], "rsin": c["rsin"]}
    for n in CONST_NAMES:
        shared["c_" + n] = c[n]
    f = lambda a: np.ascontiguousarray(np.asarray(a, dtype=np.float32))
    shared["wqkv0"] = f(inputs["a_w_qkv"][0])
    shared["wqkv1"] = f(inputs["b_w_in"][0])
    shared["wqkv2"] = f(inputs["c_w_qkv"][0])
    shared["wqkv3"] = f(inputs["a_w_qkv"][1])
    shared["wo0"] = f(inputs["a_w_o"][0])
    shared["wo1"] = f(inputs["b_w_o"][0])
    shared["wo2"] = f(inputs["c_w_o"][0])
    shared["wo3"] = f(inputs["a_w_o"][1])
    for L in range(4):
        shared["win%d" % L] = f(inputs["ffn_w_in"][L])
        shared["wdn%d" % L] = f(inputs["ffn_w_down"][L])
    for L, ia in ((0, 0), (3, 1)):
        rb = np.asarray(inputs["a_rel_bias"][ia], np.float32)
        st = rb[:, idx]
        st = np.where(band[None], st, np.float32(NEG))
        shared["strip%d" % L] = np.ascontiguousarray(st.transpose(1, 0, 2))
    x = np.asarray(inputs["x"], np.float32)
    maps = []
    for b in range(8):
        m = dict(shared)
        m["x"] = np.ascontiguousarray(x[b])
        maps.append(m)
    return maps


def kernel(**inputs):
    inputs = {k: np.asarray(v) for k, v in inputs.items()}
    if "nc" not in _CACHE:
        _CACHE["nc"] = build(4)
    nc = _CACHE["nc"]
    maps = make_in_maps(inputs)
    res = run_bass_kernel_spmd(nc, maps, core_ids=list(range(8)))
    out = np.stack([np.asarray(r["y"], np.float32) for r in res.results], axis=0)
    return out
```
